# Optimizing a Trainium2 kernel written in Bass

```python
import math, functools
import jax, jax.numpy as jnp
from jax import lax
import numpy as np

D_MODEL = 2048
BATCH = 4
SEQ = 2048
DEPTH = 2
DEC_BATCH = 8
DEC_SEQ = 8
PAST_LEN = 16384
PAGE_SIZE = 128

HEAD_DIM = 128
ATT_HEADS = (3 * D_MODEL) // (8 * HEAD_DIM)
ATT_W = ATT_HEADS * HEAD_DIM
IDX_HEADS = 16
IDX_DIM = 64
TOPK_MAX = 256
QBLOCK = 128
SSM_GROUP = 16
SSM_W = (3 * D_MODEL) // 8
SSM_GROUPS = SSM_W // SSM_GROUP
SSM_STATE = 64
CROSS_HEADS = 4
CROSS_HD = D_MODEL // 16
CROSS_W = CROSS_HEADS * CROSS_HD
MEM_LEN = 256
REL_BUCKETS = 32
REL_MAX_DIST = 128
N_BRANCHES = 3
EPS = 1e-6
_SPLIT_WIDTHS = (ATT_W, ATT_W, ATT_W, ATT_W, IDX_HEADS * IDX_DIM, IDX_HEADS, IDX_DIM,
                 SSM_W, SSM_W, CROSS_W, CROSS_W, N_BRANCHES * D_MODEL)
IN_WIDTH = sum(_SPLIT_WIDTHS)
SPLIT_POINTS = tuple(int(c) for c in np.cumsum(_SPLIT_WIDTHS)[:-1])

kernel_name = 'gated_dsa_s5_memory_decoder_step'


def rms_norm(x, g):
    x32 = x.astype(jnp.float32)
    y = x32 * lax.rsqrt(jnp.mean(x32 * x32, axis=-1, keepdims=True) + EPS)
    return (y * g.astype(jnp.float32)).astype(x.dtype)


def t5_bucket(dist):
    n = jnp.maximum(dist, 0)
    max_exact = REL_BUCKETS // 2
    nf = jnp.maximum(n, 1).astype(jnp.float32)
    large = max_exact + (jnp.log(nf / max_exact) / math.log(REL_MAX_DIST / max_exact)
                         * (REL_BUCKETS - max_exact)).astype(jnp.int32)
    large = jnp.minimum(large, REL_BUCKETS - 1)
    return jnp.where(n < max_exact, n, large)


def dsa_select(qi, wi, kidx, q_pos, topk):
    s = jnp.einsum('bqhd,bld->bqhl', qi, kidx)
    score = jnp.einsum('bqhl,bqh->bql', jax.nn.relu(s).astype(jnp.float32), wi.astype(jnp.float32))
    key_pos = jnp.arange(kidx.shape[1])
    admissible = key_pos[None, None, :] <= q_pos[None, :, None]
    score = jnp.where(admissible, score, -jnp.inf)
    _, idx = lax.top_k(score, topk)
    valid = idx <= q_pos[None, :, None]
    return idx, valid


def sparse_softmax(q, k_sel, v_sel, idx, valid, q_pos, rel_bias):
    logits = jnp.einsum('bqhd,bqkhd->bqhk', q, k_sel).astype(jnp.float32) * (HEAD_DIM ** -0.5)
    bias = rel_bias[t5_bucket(q_pos[None, :, None] - idx)]
    logits = logits + jnp.moveaxis(bias, -1, 2).astype(jnp.float32)
    logits = jnp.where(valid[:, :, None, :], logits, -jnp.inf)
    p = jax.nn.softmax(logits, axis=-1).astype(v_sel.dtype)
    return jnp.einsum('bqhk,bqkhd->bqhd', p, v_sel)


def dsa_prompt(rel_bias, q, k, v, qi, wi, kidx):
    B, S = q.shape[0], q.shape[1]
    nb = S // QBLOCK
    topk = min(TOPK_MAX, S // 4)
    bidx = jnp.arange(B)[:, None, None]

    def blocks(a):
        return jnp.moveaxis(a.reshape((B, nb, QBLOCK) + a.shape[2:]), 1, 0)

    def one_block(args):
        qb, qib, wb, t0 = args
        q_pos = t0 + jnp.arange(QBLOCK)
        idx, valid = dsa_select(qib, wb, kidx, q_pos, topk)
        return sparse_softmax(qb, k[bidx, idx], v[bidx, idx], idx, valid, q_pos, rel_bias)

    out = lax.map(one_block, (blocks(q), blocks(qi), blocks(wi), jnp.arange(nb) * QBLOCK))
    return jnp.moveaxis(out, 0, 1).reshape(B, S, ATT_HEADS, HEAD_DIM)


def dsa_sample(pool_k, pool_v, pool_kidx, page_table, rel_bias, q, k_new, v_new, qi, wi, kidx_new):
    DB, T = q.shape[0], q.shape[1]
    page = pool_k.shape[1]
    past = page_table.shape[1] * page
    kidx_past = pool_kidx[page_table].reshape(DB, past, IDX_DIM)
    kidx_all = jnp.concatenate([kidx_past, kidx_new.astype(kidx_past.dtype)], axis=1)
    topk = min(TOPK_MAX, (past + T) // 4)
    q_pos = past + jnp.arange(T)
    idx, valid = dsa_select(qi, wi, kidx_all, q_pos, topk)
    bidx = jnp.arange(DB)[:, None, None]
    pc = jnp.minimum(idx, past - 1)
    phys = page_table[bidx, pc // page]
    off = pc % page
    nc = jnp.clip(idx - past, 0, T - 1)
    is_new = (idx >= past)[..., None, None]
    k_sel = jnp.where(is_new, k_new[bidx, nc].astype(pool_k.dtype), pool_k[phys, off])
    v_sel = jnp.where(is_new, v_new[bidx, nc].astype(pool_v.dtype), pool_v[phys, off])
    return sparse_softmax(q, k_sel, v_sel, idx, valid, q_pos, rel_bias)


def memory_kv(mem, w_mem_kv):
    B, M = mem.shape[0], mem.shape[1]
    k, v = jnp.split(mem @ w_mem_kv, 2, axis=-1)
    return k.reshape(B, M, CROSS_HEADS, CROSS_HD), v.reshape(B, M, CROSS_HEADS, CROSS_HD)


def cross_attend(q, mem_k, mem_v):
    B, S = q.shape[0], q.shape[1]
    logits = jnp.einsum('bshd,bmhd->bhsm', q, mem_k.astype(q.dtype)).astype(jnp.float32) * (CROSS_HD ** -0.5)
    p = jax.nn.softmax(logits, axis=-1).astype(q.dtype)
    return jnp.einsum('bhsm,bmhd->bshd', p, mem_v.astype(q.dtype)).reshape(B, S, CROSS_W)


def _ssm_combine(e1, e2):
    a1r, a1i, b1r, b1i = e1
    a2r, a2i, b2r, b2i = e2
    return (a2r * a1r - a2i * a1i,
            a2r * a1i + a2i * a1r,
            a2r * b1r - a2i * b1i + b2r,
            a2r * b1i + a2i * b1r + b2i)


def s5_branch(u, h0_re, h0_im, a_re, a_im, b_re, b_im, c_re, c_im, d_skip, log_dt, w_glu, b_glu):
    f32 = jnp.float32
    B, S = u.shape[0], u.shape[1]
    u32 = u.astype(f32)
    ug = u32.reshape(B, S, SSM_GROUPS, SSM_GROUP)
    lam_re = jnp.minimum(a_re.astype(f32), -1e-4)
    lam_im = a_im.astype(f32)
    step = jnp.exp(log_dt.astype(f32))[:, None]
    mag = jnp.exp(lam_re * step)
    ang = lam_im * step
    lb_re, lb_im = mag * jnp.cos(ang), mag * jnp.sin(ang)
    nr, ni = lb_re - 1.0, lb_im
    den = lam_re * lam_re + lam_im * lam_im
    coef_re = (nr * lam_re + ni * lam_im) / den
    coef_im = (ni * lam_re - nr * lam_im) / den
    br, bi = b_re.astype(f32), b_im.astype(f32)
    bb_re = coef_re[..., None] * br - coef_im[..., None] * bi
    bb_im = coef_re[..., None] * bi + coef_im[..., None] * br
    bu_re = jnp.einsum('bsgc,gpc->bsgp', ug, bb_re)
    bu_im = jnp.einsum('bsgc,gpc->bsgp', ug, bb_im)
    h0r, h0i = h0_re.astype(f32), h0_im.astype(f32)
    bu_re = bu_re.at[:, 0].add(lb_re * h0r - lb_im * h0i)
    bu_im = bu_im.at[:, 0].add(lb_re * h0i + lb_im * h0r)
    a_r = jnp.broadcast_to(lb_re, bu_re.shape)
    a_i = jnp.broadcast_to(lb_im, bu_im.shape)
    _, _, x_re, x_im = lax.associative_scan(_ssm_combine, (a_r, a_i, bu_re, bu_im), axis=1)
    y = (jnp.einsum('gcp,bsgp->bsgc', c_re.astype(f32), x_re)
         - jnp.einsum('gcp,bsgp->bsgc', c_im.astype(f32), x_im)).reshape(B, S, SSM_W)
    y = y + d_skip.astype(f32) * u32
    z = jax.nn.gelu(y)
    ga, gb = jnp.split(z @ w_glu.astype(f32) + b_glu.astype(f32), 2, axis=-1)
    out = ga * jax.nn.sigmoid(gb)
    return out.astype(u.dtype), x_re[:, -1], x_im[:, -1]


def mixer_layer(x, attend, h0_re, h0_im, mem_k, mem_v, norm_g, w_in, w_ba, w_bs, w_bc, w_out, ssm_params):
    B, S = x.shape[0], x.shape[1]
    h = rms_norm(x, norm_g)
    q, k, v, g_a, qi, wi, ki, u, g_s, qc, g_c, g_m = jnp.split(h @ w_in, SPLIT_POINTS, axis=-1)
    q = q.reshape(B, S, ATT_HEADS, HEAD_DIM)
    k = k.reshape(B, S, ATT_HEADS, HEAD_DIM)
    v = v.reshape(B, S, ATT_HEADS, HEAD_DIM)
    qi = qi.reshape(B, S, IDX_HEADS, IDX_DIM)
    wi = wi * (IDX_HEADS ** -0.5)
    a_out = attend(q, k, v, qi, wi, ki).reshape(B, S, ATT_W) * jax.nn.silu(g_a)
    s_out, h_re, h_im = s5_branch(u, h0_re, h0_im, *ssm_params)
    s_out = s_out * jax.nn.silu(g_s)
    c_out = cross_attend(qc.reshape(B, S, CROSS_HEADS, CROSS_HD), mem_k, mem_v) * jax.nn.silu(g_c)
    gates = jax.nn.sigmoid(g_m).reshape(B, S, N_BRANCHES, D_MODEL)
    merged = (gates[:, :, 0] * (a_out @ w_ba) + gates[:, :, 1] * (s_out @ w_bs)
              + gates[:, :, 2] * (c_out @ w_bc))
    return x + merged @ w_out, (k, v, ki, h_re, h_im)


def setup_inputs(seed: int = 0) -> dict:
    key = jax.random.key(seed)
    ks = jax.random.split(key, 32)
    f32 = jnp.float32
    n_pages = PAST_LEN // PAGE_SIZE
    n_used = DEC_BATCH * n_pages
    n_pool = n_used + max(1, n_used // 4)
    nrm = lambda k, shape, s=1.0: jax.random.normal(k, shape, f32) * s
    page_table = jax.random.permutation(ks[0], n_pool)[:n_used].reshape(DEC_BATCH, n_pages).astype(jnp.int32)
    a_im = (jnp.pi * jnp.arange(SSM_STATE, dtype=f32))[None, None, :] + nrm(ks[20], (DEPTH, SSM_GROUPS, SSM_STATE), 0.01)
    return {
        'x_prompt': nrm(ks[1], (BATCH, SEQ, D_MODEL)),
        'x_sample': nrm(ks[2], (DEC_BATCH, DEC_SEQ, D_MODEL)),
        'cache_k': nrm(ks[3], (DEPTH, n_pool, PAGE_SIZE, ATT_HEADS, HEAD_DIM)),
        'cache_v': nrm(ks[4], (DEPTH, n_pool, PAGE_SIZE, ATT_HEADS, HEAD_DIM)),
        'cache_kidx': nrm(ks[5], (DEPTH, n_pool, PAGE_SIZE, IDX_DIM)),
        'cache_mem_k': nrm(ks[6], (DEPTH, DEC_BATCH, MEM_LEN, CROSS_HEADS, CROSS_HD)),
        'cache_mem_v': nrm(ks[7], (DEPTH, DEC_BATCH, MEM_LEN, CROSS_HEADS, CROSS_HD)),
        'state_ssm_re': nrm(ks[8], (DEPTH, DEC_BATCH, SSM_GROUPS, SSM_STATE), 0.5),
        'state_ssm_im': nrm(ks[9], (DEPTH, DEC_BATCH, SSM_GROUPS, SSM_STATE), 0.5),
        'page_table': page_table,
        'mem_prompt': nrm(ks[10], (BATCH, MEM_LEN, D_MODEL)),
        'norm_g': 1.0 + nrm(ks[11], (DEPTH, D_MODEL), 0.02),
        'w_in': nrm(ks[12], (DEPTH, D_MODEL, IN_WIDTH), D_MODEL ** -0.5),
        'w_branch_attn': nrm(ks[13], (DEPTH, ATT_W, D_MODEL), ATT_W ** -0.5),
        'w_branch_ssm': nrm(ks[14], (DEPTH, SSM_W, D_MODEL), SSM_W ** -0.5),
        'w_branch_cross': nrm(ks[15], (DEPTH, CROSS_W, D_MODEL), CROSS_W ** -0.5),
        'w_out': nrm(ks[16], (DEPTH, D_MODEL, D_MODEL), D_MODEL ** -0.5),
        'w_mem_kv': nrm(ks[17], (DEPTH, D_MODEL, 2 * CROSS_W), D_MODEL ** -0.5),
        'ssm_a_re': -0.5 * jnp.exp(nrm(ks[18], (DEPTH, SSM_GROUPS, SSM_STATE), 0.05)),
        'ssm_a_im': a_im,
        'ssm_b_re': nrm(ks[19], (DEPTH, SSM_GROUPS, SSM_STATE, SSM_GROUP), (2 * SSM_GROUP) ** -0.5),
        'ssm_b_im': nrm(ks[21], (DEPTH, SSM_GROUPS, SSM_STATE, SSM_GROUP), (2 * SSM_GROUP) ** -0.5),
        'ssm_c_re': nrm(ks[22], (DEPTH, SSM_GROUPS, SSM_GROUP, SSM_STATE), SSM_STATE ** -0.5),
        'ssm_c_im': nrm(ks[23], (DEPTH, SSM_GROUPS, SSM_GROUP, SSM_STATE), SSM_STATE ** -0.5),
        'ssm_d': nrm(ks[24], (DEPTH, SSM_W)),
        'ssm_log_dt': jax.random.uniform(ks[25], (DEPTH, SSM_GROUPS), f32, math.log(1e-3), math.log(1e-1)),
        'w_glu': nrm(ks[26], (DEPTH, SSM_W, 2 * SSM_W), SSM_W ** -0.5),
        'b_glu': nrm(ks[27], (DEPTH, 2 * SSM_W), 0.01),
        'rel_bias': nrm(ks[28], (REL_BUCKETS, ATT_HEADS), 0.5),
        'final_norm_g': 1.0 + nrm(ks[29], (D_MODEL,), 0.02),
    }


def reference(x_prompt, x_sample, cache_k, cache_v, cache_kidx, cache_mem_k, cache_mem_v,
              state_ssm_re, state_ssm_im, page_table, mem_prompt, norm_g, w_in, w_branch_attn,
              w_branch_ssm, w_branch_cross, w_out, w_mem_kv, ssm_a_re, ssm_a_im, ssm_b_re, ssm_b_im,
              ssm_c_re, ssm_c_im, ssm_d, ssm_log_dt, w_glu, b_glu, rel_bias, final_norm_g):
    yp, ys = x_prompt, x_sample
    B = x_prompt.shape[0]
    zeros_h = jnp.zeros((B, SSM_GROUPS, SSM_STATE), jnp.float32)
    kp_l, vp_l, kip_l, mkp_l, mvp_l, srp_l, sip_l = [], [], [], [], [], [], []
    ks_l, vs_l, kis_l, srs_l, sis_l = [], [], [], [], []
    attend_p = functools.partial(dsa_prompt, rel_bias)
    for l in range(DEPTH):
        layer_w = (norm_g[l], w_in[l], w_branch_attn[l], w_branch_ssm[l], w_branch_cross[l], w_out[l])
        ssm_l = (ssm_a_re[l], ssm_a_im[l], ssm_b_re[l], ssm_b_im[l], ssm_c_re[l], ssm_c_im[l],
                 ssm_d[l], ssm_log_dt[l], w_glu[l], b_glu[l])
        mk_p, mv_p = memory_kv(mem_prompt, w_mem_kv[l])
        yp, (kp, vp, kip, srp, sip) = mixer_layer(yp, attend_p, zeros_h, zeros_h, mk_p, mv_p, *layer_w, ssm_l)
        kp_l.append(kp); vp_l.append(vp); kip_l.append(kip); mkp_l.append(mk_p); mvp_l.append(mv_p)
        srp_l.append(srp); sip_l.append(sip)
        attend_s = functools.partial(dsa_sample, cache_k[l], cache_v[l], cache_kidx[l], page_table, rel_bias)
        ys, (kn, vn, kin, srs, sis) = mixer_layer(ys, attend_s, state_ssm_re[l], state_ssm_im[l],
                                                  cache_mem_k[l], cache_mem_v[l], *layer_w, ssm_l)
        ks_l.append(kn); vs_l.append(vn); kis_l.append(kin); srs_l.append(srs); sis_l.append(sis)
    y_prompt = rms_norm(yp, final_norm_g)
    y_sample = rms_norm(ys, final_norm_g)
    st = lambda xs: jnp.stack(xs, axis=0)
    return (y_prompt, y_sample, st(kp_l), st(vp_l), st(kip_l), st(mkp_l), st(mvp_l), st(srp_l), st(sip_l),
            st(ks_l), st(vs_l), st(kis_l), st(srs_l), st(sis_l))
```

```python
import math
from contextlib import ExitStack
import numpy as np
import concourse.bass as bass
import concourse.mybir as mybir
from concourse.bass_utils import run_bass_kernel_spmd

F32 = mybir.dt.float32; BF16 = mybir.dt.bfloat16; I32 = mybir.dt.int32
AF = mybir.ActivationFunctionType; ALU = mybir.AluOpType; AX = mybir.AxisListType

D = 2048; SEQ = 2048; T = 512; NTILE = 4; DEPTH = 2; NS = 8
ATT_H = 6; IDX_H = 16; IDX_D = 64; CR_H = 4; MEM = 256
PAST = 16384; NPAGE = 128; NPOOL = 1280
O_Q, O_K, O_V, O_GA, O_QI, O_WI, O_KI, O_U, O_GS, O_QC, O_GC, O_GM, O_END = (
    0, 768, 1536, 2304, 3072, 4096, 4112, 4176, 4944, 5712, 6224, 6736, 12880)
EPS = 1e-6
NEG = -30000.0
BIS_R = 512.0
BIS_IT = 24
SEM_LIMIT = 24000

SAMPLE = True
PROMPT = True
STOPAT = 9
VAR = 0
NL = DEPTH
NT = NTILE


class Sched:
    ENG = ['pe', 'act', 'dve', 'pool', 'sp']

    def __init__(self, nc, es, ndma=16):
        self.nc = nc; self.es = es; self.ndma = ndma
        self.eng = {'pe': nc.tensor, 'act': nc.scalar, 'dve': nc.vector, 'pool': nc.gpsimd, 'sp': nc.sync}
        self.epoch = 0
        self.ninst = 0
        self._fresh()

    def _fresh(self):
        es = self.es; nc = self.nc; ep = self.epoch
        self.sem = {e: es.enter_context(nc.semaphore(f"s{ep}_{e}")) for e in self.ENG}
        self.cnt = {e: 0 for e in self.ENG}
        self.dsem = [es.enter_context(nc.semaphore(f"d{ep}_{i}")) for i in range(self.ndma)]
        self.dcnt = [0] * self.ndma
        self.dnext = 0
        self.waited = {e: {} for e in self.ENG}
        self.lastw = {}
        self.readers = {}

    def _wait(self, e, tok):
        skey, val, prod = tok
        if prod == e and e == 'pe':
            return
        if self.waited[e].get(skey, 0) >= val:
            return
        sem = self.sem[skey] if isinstance(skey, str) else self.dsem[skey]
        self.eng[e].wait_ge(sem, val)
        self.waited[e][skey] = val
        self.ninst += 1

    def _deps(self, e, reads, writes):
        for k in reads:
            t = self.lastw.get(k)
            if t is not None:
                self._wait(e, t)
        for k in writes:
            t = self.lastw.get(k)
            if t is not None:
                self._wait(e, t)
            for t in self.readers.get(k, ()):
                self._wait(e, t)

    def _record(self, tok, reads, writes):
        for k in reads:
            self.readers.setdefault(k, []).append(tok)
        for k in writes:
            self.lastw[k] = tok
            self.readers[k] = []

    def op(self, e, fn, reads=(), writes=()):
        writes = list(writes) + [k for k in reads if isinstance(k, str) and k.startswith('ps') and k not in writes]
        self._deps(e, reads, writes)
        inst = fn(self.eng[e])
        self.cnt[e] += 1
        inst.then_inc(self.sem[e], 1)
        self._record((e, self.cnt[e], e), reads, writes)
        self.ninst += 1
        return inst

    def _dma(self, e, mk, reads, writes):
        i = self.dnext
        self.dnext = (self.dnext + 1) % len(self.dsem)
        if self.dcnt[i] > 0:
            self._wait(e, (i, 16 * self.dcnt[i], None))
        self._deps(e, reads, writes)
        inst = mk()
        self.dcnt[i] += 1
        inst.then_inc(self.dsem[i], 16)
        self._record((i, 16 * self.dcnt[i], None), reads, writes)
        self.ninst += 1
        return inst

    def dma(self, e, out, in_, reads=(), writes=(), **kw):
        return self._dma(e, lambda: self.eng[e].dma_start(out=out, in_=in_, **kw), reads, writes)

    def idma(self, out, table, idx_ap, reads=(), writes=()):
        return self._dma('pool', lambda: self.nc.gpsimd.indirect_dma_start(
            out=out, out_offset=None, in_=table, in_offset=bass.IndirectOffsetOnAxis(ap=idx_ap, axis=0)), reads, writes)

    def barrier(self, engs=None):
        for e in (engs or self.ENG):
            for i in range(len(self.dsem)):
                if self.dcnt[i] > 0:
                    self._wait(e, (i, 16 * self.dcnt[i], None))
            for e2 in self.ENG:
                if e2 != e and self.cnt[e2] > 0:
                    self._wait(e, (e2, self.cnt[e2], e2))
        if engs is None and (max(self.cnt.values()) > SEM_LIMIT or 16 * max(self.dcnt) > SEM_LIMIT):
            self.epoch += 1
            self._fresh()


def t5_bucket_np(n):
    n = np.maximum(n, 0)
    max_exact = 16
    nf = np.maximum(n, 1).astype(np.float32)
    large = max_exact + (np.log(nf / max_exact) / math.log(128 / max_exact) * (32 - max_exact)).astype(np.int32)
    large = np.minimum(large, 31)
    return np.where(n < max_exact, n, large)


def make_consts():
    ident = np.eye(128, dtype=np.float32)
    q = np.arange(128)[:, None]; l = np.arange(128)[None, :]
    cmask = np.where(l <= q, 0.0, -1e30).astype(np.float32)
    oh = np.zeros((128, 2, 32, 128), np.float32)
    for dlt in range(2):
        b = t5_bucket_np(q - l + 128 * dlt)
        for bb in range(32):
            oh[:, dlt, bb, :] = (b == bb)
    ohs = np.zeros((NS, 32, 136), np.float32)
    for t in range(NS):
        for off in range(128):
            ohs[t, int(t5_bucket_np(np.array(128 + t - off))), off] = 1.0
        for tk in range(t + 1):
            ohs[t, int(t5_bucket_np(np.array(t - tk))), 128 + tk] = 1.0
    iota = np.arange(128, dtype=np.float32).reshape(128, 1)
    qm = np.zeros((128, 4), np.float32)
    for r in range(128):
        qm[r, r // 32] = 1.0
    return {"c_ident": ident, "c_cmask": cmask, "c_oh": oh.reshape(128, 2 * 32 * 128),
            "c_ohs": ohs.reshape(NS, 32 * 136), "c_iota": iota, "c_qm": qm}


class NS_:
    pass


def build_program():
    nc = bass.Bass("TRN2", target_bir_lowering=False)
    uid = [0]

    def din(name, shape, dt=F32):
        return nc.dram_tensor(name, list(shape), dt, kind="ExternalInput").ap()

    def dout(name, shape, dt=F32):
        return nc.dram_tensor(name, list(shape), dt, kind="ExternalOutput").ap()

    xp = din("xp", [SEQ, D]); xs_in = din("xs", [NS, D])
    memp = din("memp", [MEM, D])
    norm_g = din("norm_g", [DEPTH, D]); w_in = din("w_in", [DEPTH, D, O_END])
    w_ba = din("w_ba", [DEPTH, 768, D]); w_bs = din("w_bs", [DEPTH, 768, D]); w_bc = din("w_bc", [DEPTH, 512, D])
    w_out = din("w_out", [DEPTH, D, D]); w_mem = din("w_mem", [DEPTH, D, 1024])
    a_re = din("a_re", [DEPTH, 48, 64]); a_im = din("a_im", [DEPTH, 48, 64])
    b_re = din("b_re", [DEPTH, 48, 64, 16]); b_im = din("b_im", [DEPTH, 48, 64, 16])
    c_re = din("c_re", [DEPTH, 48, 16, 64]); c_im = din("c_im", [DEPTH, 48, 16, 64])
    ssm_d = din("ssm_d", [DEPTH, 768]); log_dt = din("log_dt", [DEPTH, 48])
    w_glu = din("w_glu", [DEPTH, 768, 1536]); b_glu = din("b_glu", [DEPTH, 1536])
    rel_bias = din("rel_bias", [32, 6]); fng = din("fng", [D])
    c_ident = din("c_ident", [128, 128]); c_cmask = din("c_cmask", [128, 128]); c_oh = din("c_oh", [128, 2 * 32 * 128])
    c_ohs = din("c_ohs", [NS, 32 * 136]); c_iota = din("c_iota", [128, 1]); c_qm = din("c_qm", [128, 4])
    if SAMPLE:
        ck = [din(f"ck{l}", [NPOOL * 128, 768]) for l in range(DEPTH)]
        cv = [din(f"cv{l}", [NPOOL * 128, 768]) for l in range(DEPTH)]
        cki = [din(f"cki{l}", [NPOOL * 128, 64]) for l in range(DEPTH)]
    cmk = din("cmk", [DEPTH, MEM, 512]); cmv = din("cmv", [DEPTH, MEM, 512])
    sre = din("sre", [DEPTH, 48, 64]); sim_ = din("sim", [DEPTH, 48, 64])
    pgt = din("pgt", [1, NPAGE], I32)

    yp = dout("yp", [SEQ, D]); ys = dout("ys", [NS, D])
    kp = dout("kp", [DEPTH, SEQ, 768]); vp = dout("vp", [DEPTH, SEQ, 768]); kip = dout("kip", [DEPTH, SEQ, 64])
    mkp = dout("mkp", [DEPTH, MEM, 512]); mvp = dout("mvp", [DEPTH, MEM, 512])
    srp = dout("srp", [DEPTH, 48, 64]); sip = dout("sip", [DEPTH, 48, 64])
    ks = dout("ks", [DEPTH, NS, 768]); vs = dout("vs", [DEPTH, NS, 768]); kis = dout("kis", [DEPTH, NS, 64])
    srs = dout("srs", [DEPTH, 48, 64]); sis = dout("sis", [DEPTH, 48, 64])
    xscr = nc.dram_tensor("xscr", [2, SEQ, D], F32, kind="Internal").ap()
    xsscr = nc.dram_tensor("xsscr", [2, NS, D], F32, kind="Internal").ap()

    with ExitStack() as es:
        S = Sched(nc, es)
        NCD = dict(allow_slow_non_contiguous=True)

        def sbt(stack, shape, dt, nm="t"):
            uid[0] += 1
            return stack.enter_context(nc.sbuf_tensor(f"{nm}{uid[0]}", list(shape), dt))

        def pst(stack, shape, dt, nm="p"):
            uid[0] += 1
            return stack.enter_context(nc.psum_tensor(f"{nm}{uid[0]}", list(shape), dt))

        ident_f = sbt(es, [128, 128], F32); ident_b = sbt(es, [128, 128], BF16)
        ones_b = sbt(es, [128, 128], BF16); cmask = sbt(es, [128, 128], F32)
        biasT = sbt(es, [128, ATT_H, 3, 128], BF16)
        biasS = sbt(es, [128, ATT_H, 136], BF16)
        idx_i = sbt(es, [128, NPAGE], I32)
        qm2 = sbt(es, [128, 4], F32)
        wbufs = [sbt(es, [128, 8192], BF16, "wb") for _ in range(2)]
        wstate = [0]
        psg = [pst(es, [128, 512], F32) for _ in range(4)]
        psa = [pst(es, [128, 512], F32) for _ in range(2)]
        psb = [pst(es, [128, 1024], BF16) for _ in range(2)]
        psgi = [0]; evi = [0]

        def nextps():
            i = psgi[0]; psgi[0] = (i + 1) % 4
            return psg[i], f"psg{i}"

        def evac_eng():
            evi[0] ^= 1
            return 'act' if evi[0] else 'dve'

        def copy(eng, out, in_, reads, writes):
            if eng == 'act':
                S.op('act', lambda e: e.activation(out=out, in_=in_, func=AF.Copy), reads=reads, writes=writes)
            else:
                S.op(eng, lambda e: e.tensor_copy(out=out, in_=in_), reads=reads, writes=writes)

        def wload(src2d, nk, c0, C):
            i = wstate[0]; wstate[0] ^= 1
            view = wbufs[i][:, 0:nk * C].rearrange("p (k c) -> p k c", c=C)
            S.dma('pool', view, src2d.rearrange("(k p) c -> p k c", p=128)[:, :, c0:c0 + C], writes=[f"wb{i}"])
            return view, f"wb{i}"

        def rms_rstd(src_rows, junk_rows, ssq_rows, rk):
            S.op('act', lambda e: e.activation(out=junk_rows, in_=src_rows, func=AF.Square, accum_out=ssq_rows), reads=[rk], writes=['junk', 'ssq'])
            S.op('dve', lambda e: e.tensor_scalar(out=ssq_rows, in0=ssq_rows, scalar1=1.0 / D, scalar2=EPS, op0=ALU.mult, op1=ALU.add), reads=['ssq'], writes=['ssq'])
            S.op('act', lambda e: e.activation(out=ssq_rows, in_=ssq_rows, func=AF.Sqrt), reads=['ssq'], writes=['ssq'])
            S.op('dve', lambda e: e.reciprocal(out=ssq_rows, in_=ssq_rows), reads=['ssq'], writes=['ssq'])

        S.dma('sp', ident_f[:], c_ident[:, :], writes=['ident_f'])
        S.dma('sp', cmask[:], c_cmask[:, :], writes=['cmask'])
        copy('dve', ident_b[:], ident_f[:], ['ident_f'], ['ident_b'])
        S.op('dve', lambda e: e.memset(ones_b[:], 1.0), writes=['ones_b'])
        S.dma('sp', qm2[:], c_qm[:, :], writes=['qm2'])

        with ExitStack() as ph:
            oh = sbt(ph, [128, 2, 32, 128], BF16); rbB = sbt(ph, [128, 192], F32); acc = sbt(ph, [128, 136], F32)
            S.dma('sp', rbB[:], rel_bias.rearrange("b h -> (b h)").unsqueeze(0).broadcast_to([128, 192]), writes=['rbB'])
            if PROMPT:
                S.dma('pool', oh[:].rearrange("p a (x y) l -> p (a x) (y l)", y=4), c_oh.rearrange("p (x c) -> p x c", c=512), writes=['oh'])
                for h in range(ATT_H):
                    for dlt in range(2):
                        for b in range(32):
                            sc = rbB[:, b * 6 + h:b * 6 + h + 1]
                            if b == 0:
                                S.op('dve', lambda e: e.tensor_scalar(out=acc[:, 0:128], in0=oh[:, dlt, b, :], scalar1=sc, scalar2=None, op0=ALU.mult),
                                     reads=['oh', 'rbB'], writes=['acc'])
                            else:
                                S.op('dve', lambda e: e.scalar_tensor_tensor(out=acc[:, 0:128], in0=oh[:, dlt, b, :], scalar=sc, in1=acc[:, 0:128], op0=ALU.mult, op1=ALU.add),
                                     reads=['oh', 'rbB', 'acc'], writes=['acc'])
                        copy('act', biasT[:, h, dlt, :], acc[:, 0:128], ['acc'], ['biasT'])
                    sc = rbB[:, 31 * 6 + h:31 * 6 + h + 1]
                    S.op('dve', lambda e: e.tensor_scalar(out=biasT[:, h, 2, :], in0=ident_f[:], scalar1=0.0, scalar2=sc, op0=ALU.mult, op1=ALU.add),
                         reads=['ident_f', 'rbB'], writes=['biasT'])
            if SAMPLE:
                ohs = sbt(ph, [128, 32, 136], F32); rbd = sbt(ph, [128, 32, 6], F32)
                pgt_i = sbt(ph, [128, NPAGE], I32); pgt_f = sbt(ph, [128, NPAGE], F32); iota_f = sbt(ph, [128, 1], F32)
                R8 = slice(0, NS)
                S.dma('sp', ohs[R8, :, :], c_ohs.rearrange("q (b k) -> q b k", k=136), writes=['ohs'])
                S.op('dve', lambda e: e.tensor_tensor(out=rbd[R8, :, :], in0=rbB[R8, :].rearrange("p (b h) -> p b h", h=6),
                                                      in1=rbB[R8, 31 * 6:32 * 6].unsqueeze(1).broadcast_to([NS, 32, 6]), op=ALU.subtract),
                     reads=['rbB'], writes=['rbd'])
                for h in range(ATT_H):
                    for b in range(32):
                        sc = rbd[R8, b, h:h + 1]
                        if b == 0:
                            S.op('dve', lambda e: e.tensor_scalar(out=acc[R8, :], in0=ohs[R8, b, :], scalar1=sc, scalar2=None, op0=ALU.mult),
                                 reads=['ohs', 'rbd'], writes=['acc'])
                        else:
                            S.op('dve', lambda e: e.scalar_tensor_tensor(out=acc[R8, :], in0=ohs[R8, b, :], scalar=sc, in1=acc[R8, :], op0=ALU.mult, op1=ALU.add),
                                 reads=['ohs', 'rbd', 'acc'], writes=['acc'])
                    copy('act', biasS[R8, h, :], acc[R8, :], ['acc'], ['biasS'])
                S.dma('sp', pgt_i[:], pgt.rearrange("a n -> (a n)").unsqueeze(0).broadcast_to([128, NPAGE]), writes=['pgt_i'])
                S.dma('sp', iota_f[:], c_iota[:, :], writes=['iota_f'])
                copy('dve', pgt_f[:], pgt_i[:], ['pgt_i'], ['pgt_f'])
                S.op('dve', lambda e: e.tensor_scalar(out=pgt_f[:], in0=pgt_f[:], scalar1=128.0, scalar2=iota_f[:, 0:1], op0=ALU.mult, op1=ALU.add),
                     reads=['pgt_f', 'iota_f'], writes=['pgt_f'])
                copy('dve', idx_i[:], pgt_f[:], ['pgt_f'], ['idx_i'])
            S.barrier()

        for l in range(NL):
            with ExitStack() as ly:
                gT = sbt(ly, [128, 16], F32)
                S.dma('sp', gT[:], norm_g[l].rearrange("(k p) -> p k", p=128), writes=['gT'], **NCD)
                mkT = sbt(ly, [128, CR_H, MEM], BF16); mv = sbt(ly, [128, 2, 512], BF16)
                stage_f = [sbt(ly, [128, 512], F32) for _ in range(2)]
                sti = [0]

                def nextstage():
                    i = sti[0]; sti[0] ^= 1
                    return stage_f[i], f"stage{i}"

                with ExitStack() as ph:
                    memT = sbt(ph, [128, 16, MEM], BF16); mt = sbt(ph, [128, D], F32); mb = sbt(ph, [128, D], BF16)
                    if PROMPT:
                        for mbk in range(2):
                            S.dma('sp', mt[:], memp[mbk * 128:(mbk + 1) * 128, :], writes=['mt'])
                            copy('act', mb[:], mt[:], ['mt'], ['mb'])
                            for half in range(2):
                                pb = psb[half]
                                for k8 in range(8):
                                    k = half * 8 + k8
                                    S.op('pe', lambda e: e.transpose(pb[:, k8 * 128:(k8 + 1) * 128], mb[:, k * 128:(k + 1) * 128], ident_b[:]),
                                         reads=['mb', 'ident_b'], writes=[f'psb{half}'])
                                copy(evac_eng(), memT[:, half * 8:half * 8 + 8, mbk * 128:(mbk + 1) * 128],
                                     pb[:].rearrange("p (k t) -> p k t", t=128), [f'psb{half}'], ['memT'])
                        for part in range(2):
                            wv, wk = wload(w_mem[l], 16, part * 512, 512)
                            for mbk in range(2):
                                pt, pk = nextps()
                                for k in range(16):
                                    S.op('pe', lambda e: e.matmul(pt[:, :], lhsT=memT[:, k, mbk * 128:(mbk + 1) * 128], rhs=wv[:, k, :], start=(k == 0), stop=(k == 15)),
                                         reads=['memT', wk], writes=[pk])
                                st, sk = nextstage()
                                copy(evac_eng(), st[:], pt[:, :], [pk], [sk])
                                S.dma('sp', (mkp if part == 0 else mvp)[l, mbk * 128:(mbk + 1) * 128, :], st[:], reads=[sk])
                                if part == 1:
                                    copy(evac_eng(), mv[:, mbk, :], pt[:, :], [pk], ['mv'])
                            if part == 0:
                                for h in range(CR_H):
                                    pt, pk = nextps()
                                    for k in range(16):
                                        S.op('pe', lambda e: e.matmul(pt[:, 0:MEM], lhsT=wv[:, k, h * 128:(h + 1) * 128], rhs=memT[:, k, :], start=(k == 0), stop=(k == 15)),
                                             reads=['memT', wk], writes=[pk])
                                    copy(evac_eng(), mkT[:, h, :], pt[:, 0:MEM], [pk], ['mkT'])
                    S.barrier()

                lr = sbt(ly, [128, 24], F32); li = sbt(ly, [128, 24], F32)
                LR2 = sbt(ly, [128, 2, 24], F32); LIs = sbt(ly, [128, 2, 24], F32)
                BB = sbt(ly, [128, 6, 2, 128], BF16)
                Cr = sbt(ly, [128, 24, 16], F32); Ci = sbt(ly, [128, 24, 16], F32)
                dsk = sbt(ly, [128, 6], F32); bgl = sbt(ly, [128, 12], F32)
                XST = sbt(ly, [128, 2, 24], F32)
                with ExitStack() as ph:
                    lamr = sbt(ph, [128, 24], F32); lami = sbt(ph, [128, 24], F32); stp = sbt(ph, [128, 24], F32)
                    t1 = sbt(ph, [128, 24], F32); t2 = sbt(ph, [128, 24], F32); t3 = sbt(ph, [128, 24], F32)
                    cs = sbt(ph, [128, 24], F32); sn = sbt(ph, [128, 24], F32); mag = sbt(ph, [128, 24], F32)
                    cr = sbt(ph, [128, 24], F32); ci = sbt(ph, [128, 24], F32)
                    Br = sbt(ph, [128, 24, 16], F32); Bi = sbt(ph, [128, 24, 16], F32)
                    Bbr = sbt(ph, [128, 24, 16], F32); Bbi = sbt(ph, [128, 24, 16], F32); Btmp = sbt(ph, [128, 24, 16], F32)
                    Bblk = [sbt(ph, [128, 24, 2, 16], F32) for _ in range(2)]
                    halfpi = sbt(ph, [128, 1], F32)
                    S.op('dve', lambda e: e.memset(halfpi[:], math.pi / 2), writes=['halfpi'])
                    for a in range(2):
                        ps_ = slice(64 * a, 64 * a + 64)
                        S.dma('sp', lamr[ps_, :], a_re[l].rearrange("(k a) p -> a p k", a=2)[a], writes=['lamr'], **NCD)
                        S.dma('sp', lami[ps_, :], a_im[l].rearrange("(k a) p -> a p k", a=2)[a], writes=['lami'], **NCD)
                        S.dma('sp', stp[ps_, :], log_dt[l].rearrange("(k a) -> a k", a=2)[a:a + 1, :].broadcast_to([64, 24]), writes=['stp'], **NCD)
                        S.dma('sp', Br[ps_, :, :], b_re[l].rearrange("(k a) p c -> a p k c", a=2)[a], writes=['Br'], **NCD)
                        S.dma('sp', Bi[ps_, :, :], b_im[l].rearrange("(k a) p c -> a p k c", a=2)[a], writes=['Bi'], **NCD)
                        for c_ in range(16):
                            S.dma('sp', Cr[ps_, :, c_], c_re[l].rearrange("(k a) c p -> a c p k", a=2)[a, c_], writes=['Cr'], **NCD)
                            S.dma('sp', Ci[ps_, :, c_], c_im[l].rearrange("(k a) c p -> a c p k", a=2)[a, c_], writes=['Ci'], **NCD)
                    S.dma('sp', dsk[:], ssm_d[l].rearrange("(k p) -> p k", p=128), writes=['dsk'], **NCD)
                    S.dma('sp', bgl[:], b_glu[l].rearrange("(k p) -> p k", p=128), writes=['bgl'], **NCD)

                    def V(out, in0, in1, op, rd, wr):
                        S.op('dve', lambda e: e.tensor_tensor(out=out, in0=in0, in1=in1, op=op), reads=rd, writes=wr)

                    S.op('act', lambda e: e.activation(out=stp[:], in_=stp[:], func=AF.Exp), reads=['stp'], writes=['stp'])
                    S.op('dve', lambda e: e.tensor_scalar(out=lamr[:], in0=lamr[:], scalar1=-1e-4, scalar2=None, op0=ALU.min), reads=['lamr'], writes=['lamr'])
                    V(t1[:], lamr[:], stp[:], ALU.mult, ['lamr', 'stp'], ['t1'])
                    S.op('act', lambda e: e.activation(out=mag[:], in_=t1[:], func=AF.Exp), reads=['t1'], writes=['mag'])
                    V(t2[:], lami[:], stp[:], ALU.mult, ['lami', 'stp'], ['t2'])
                    S.op('act', lambda e: e.activation(out=sn[:], in_=t2[:], func=AF.Sin, scale=1.0 / 16), reads=['t2'], writes=['sn'])
                    S.op('act', lambda e: e.activation(out=cs[:], in_=t2[:], func=AF.Sin, scale=1.0 / 16, bias=halfpi[:]), reads=['t2', 'halfpi'], writes=['cs'])
                    for _ in range(4):
                        V(t1[:], cs[:], cs[:], ALU.mult, ['cs'], ['t1'])
                        V(t3[:], sn[:], sn[:], ALU.mult, ['sn'], ['t3'])
                        V(sn[:], cs[:], sn[:], ALU.mult, ['cs', 'sn'], ['sn'])
                        S.op('dve', lambda e: e.tensor_scalar(out=sn[:], in0=sn[:], scalar1=2.0, scalar2=None, op0=ALU.mult), reads=['sn'], writes=['sn'])
                        V(cs[:], t1[:], t3[:], ALU.subtract, ['t1', 't3'], ['cs'])
                    V(lr[:], mag[:], cs[:], ALU.mult, ['mag', 'cs'], ['lr'])
                    V(li[:], mag[:], sn[:], ALU.mult, ['mag', 'sn'], ['li'])
                    S.op('dve', lambda e: e.tensor_scalar(out=t1[:], in0=lr[:], scalar1=-1.0, scalar2=None, op0=ALU.add), reads=['lr'], writes=['t1'])
                    V(t2[:], lamr[:], lamr[:], ALU.mult, ['lamr'], ['t2'])
                    V(t3[:], lami[:], lami[:], ALU.mult, ['lami'], ['t3'])
                    V(t2[:], t2[:], t3[:], ALU.add, ['t2', 't3'], ['t2'])
                    S.op('dve', lambda e: e.reciprocal(out=t2[:], in_=t2[:]), reads=['t2'], writes=['t2'])
                    V(cr[:], t1[:], lamr[:], ALU.mult, ['t1', 'lamr'], ['cr'])
                    V(t3[:], li[:], lami[:], ALU.mult, ['li', 'lami'], ['t3'])
                    V(cr[:], cr[:], t3[:], ALU.add, ['cr', 't3'], ['cr'])
                    V(cr[:], cr[:], t2[:], ALU.mult, ['cr', 't2'], ['cr'])
                    V(ci[:], li[:], lamr[:], ALU.mult, ['li', 'lamr'], ['ci'])
                    V(t3[:], t1[:], lami[:], ALU.mult, ['t1', 'lami'], ['t3'])
                    V(ci[:], ci[:], t3[:], ALU.subtract, ['ci', 't3'], ['ci'])
                    V(ci[:], ci[:], t2[:], ALU.mult, ['ci', 't2'], ['ci'])
                    crb = cr[:].unsqueeze(2).broadcast_to([128, 24, 16]); cib = ci[:].unsqueeze(2).broadcast_to([128, 24, 16])
                    V(Bbr[:], Br[:], crb, ALU.mult, ['Br', 'cr'], ['Bbr'])
                    V(Btmp[:], Bi[:], cib, ALU.mult, ['Bi', 'ci'], ['Btmp'])
                    V(Bbr[:], Bbr[:], Btmp[:], ALU.subtract, ['Bbr', 'Btmp'], ['Bbr'])
                    V(Bbi[:], Bi[:], crb, ALU.mult, ['Bi', 'cr'], ['Bbi'])
                    V(Btmp[:], Br[:], cib, ALU.mult, ['Br', 'ci'], ['Btmp'])
                    V(Bbi[:], Bbi[:], Btmp[:], ALU.add, ['Bbi', 'Btmp'], ['Bbi'])
                    for ri, src in enumerate((Bbr, Bbi)):
                        S.op('dve', lambda e: e.memset(Bblk[ri][:], 0.0), writes=[f'Bblk{ri}'])
                        for a in range(2):
                            ps_ = slice(64 * a, 64 * a + 64)
                            copy('dve', Bblk[ri][ps_, :, a, :], src[ps_, :, :], [f'Bb{"ri"[ri]}'], [f'Bblk{ri}'])
                        for ch in range(6):
                            pt, pk = nextps()
                            S.op('pe', lambda e: e.transpose(pt[:, 0:128], Bblk[ri][:, 4 * ch:4 * ch + 4, :, :].rearrange("p q a c -> p (q a c)"), ident_f[:]),
                                 reads=[f'Bblk{ri}', 'ident_f'], writes=[pk])
                            copy(evac_eng(), BB[:, ch, ri, :], pt[:, 0:128], [pk], ['BB'])
                    for r_ in range(2):
                        copy('dve', LR2[:, r_, :], lr[:], ['lr'], ['LR2'])
                    copy('dve', LIs[:, 0, :], li[:], ['li'], ['LIs'])
                    S.op('dve', lambda e: e.tensor_scalar(out=LIs[:, 1, :], in0=li[:], scalar1=-1.0, scalar2=None, op0=ALU.mult), reads=['li'], writes=['LIs'])
                    S.barrier()

                def run_tile(smp, t, B):
                    N = NS if smp else T
                    sbw = NS if smp else 128
                    nsb = 1 if smp else 4
                    tok0 = 0 if smp else t * T
                    RS = slice(0, sbw)
                    hT, fmt, xg, wiT = B.hT, B.fmt, B.xg, B.wiT
                    if smp:
                        xsrc = xs_in if l == 0 else xsscr[0]
                        xdst = xsscr[l]
                    else:
                        xsrc = xp if l == 0 else xscr[0]
                        xdst = xscr[l]

                    with ExitStack() as ph:
                        xt = sbt(ph, [128, D], F32); xb = sbt(ph, [128, D], BF16); junk = sbt(ph, [128, D], BF16)
                        ssq = sbt(ph, [128, 1], F32)
                        for sb in range(nsb):
                            r0 = tok0 + sb * sbw
                            S.dma('sp', xt[RS, :], xsrc[r0:r0 + sbw, :], writes=['xt'])
                            srct = xt; rk = 'xt'
                            rms_rstd(srct[RS, :], junk[RS, :], ssq[RS, :], rk)
                            S.op('act', lambda e: e.activation(out=xb[RS, :], in_=srct[RS, :], func=AF.Copy, scale=ssq[RS, :]), reads=[rk, 'ssq'], writes=['xb'])
                            for half in range(2):
                                pb = psb[half]
                                for k8 in range(8):
                                    k = half * 8 + k8
                                    S.op('pe', lambda e: e.transpose(pb[:, k8 * sbw:(k8 + 1) * sbw], xb[RS, k * 128:(k + 1) * 128], ident_b[RS, RS]),
                                         reads=['xb', 'ident_b'], writes=[f'psb{half}'])
                                S.op('dve', lambda e: e.tensor_tensor(out=hT[:, half * 8:half * 8 + 8, sb * sbw:(sb + 1) * sbw],
                                                                      in0=pb[:, 0:8 * sbw].rearrange("p (k t) -> p k t", t=sbw),
                                                                      in1=gT[:, half * 8:half * 8 + 8].unsqueeze(2).broadcast_to([128, 8, sbw]), op=ALU.mult),
                                     reads=[f'psb{half}', 'gT'], writes=['hT'])
                        S.barrier()

                    def fm_chunk(wv, wk, cl, ncols, evac, rhsT=hT, rkey='hT', nk=16, rsl=None):
                        pt, pk = nextps()
                        for k in range(nk):
                            rr = rhsT[:, k, 0:N] if rsl is None else rhsT[:, rsl + k, 0:N]
                            S.op('pe', lambda e: e.matmul(pt[0:ncols, 0:N], lhsT=wv[:, k, cl:cl + ncols], rhs=rr, start=(k == 0), stop=(k == nk - 1)),
                                 reads=[rkey, wk], writes=[pk])
                        evac(pt, pk)

                    def tm_chunk(wv, wk, cl, ncols, sb, evac):
                        pt, pk = nextps()
                        for k in range(16):
                            S.op('pe', lambda e: e.matmul(pt[RS, 0:ncols], lhsT=hT[:, k, sb * sbw:(sb + 1) * sbw], rhs=wv[:, k, cl:cl + ncols], start=(k == 0), stop=(k == 15)),
                                 reads=['hT', wk], writes=[pk])
                        evac(pt, pk)

                    for (c0, C) in ((O_K, 512), (O_K + 512, 256)):
                        wv, wk = wload(w_in[l], 16, c0, C)
                        for cc in range(C // 128):
                            h = (c0 - O_K) // 128 + cc
                            kdst = B.KTs[:, h, 0:NS] if smp else B.KT[:, h, tok0:tok0 + T]
                            fm_chunk(wv, wk, cc * 128, 128,
                                     lambda pt, pk: copy(evac_eng(), kdst, pt[:, 0:N], [pk], ['KT']))
                        for sb in range(nsb):
                            def ev(pt, pk):
                                st, sk = nextstage()
                                copy(evac_eng(), st[RS, 0:C], pt[RS, 0:C], [pk], [sk])
                                dst = ks[l, :, c0 - O_K:c0 - O_K + C] if smp else kp[l, tok0 + sb * 128:tok0 + sb * 128 + 128, c0 - O_K:c0 - O_K + C]
                                S.dma('sp', dst, st[RS, 0:C], reads=[sk])
                            tm_chunk(wv, wk, 0, C, sb, ev)
                    for (c0, C) in ((O_V, 512), (O_V + 512, 256)):
                        wv, wk = wload(w_in[l], 16, c0, C)
                        for sb in range(nsb):
                            def ev(pt, pk):
                                st, sk = nextstage()
                                copy('act', st[RS, 0:C], pt[RS, 0:C], [pk], [sk])
                                dst = vs[l, :, c0 - O_V:c0 - O_V + C] if smp else vp[l, tok0 + sb * 128:tok0 + sb * 128 + 128, c0 - O_V:c0 - O_V + C]
                                S.dma('sp', dst, st[RS, 0:C], reads=[sk])
                                vdst = B.Vs[RS, c0 - O_V:c0 - O_V + C] if smp else B.Vc[:, t * 4 + sb, c0 - O_V:c0 - O_V + C]
                                copy('dve', vdst, pt[RS, 0:C], [pk], ['Vc'])
                            tm_chunk(wv, wk, 0, C, sb, ev)
                    wv, wk = wload(w_in[l], 16, O_WI, 80)
                    for sb in range(nsb):
                        def ev(pt, pk):
                            S.op('dve', lambda e: e.tensor_scalar(out=wiT[RS, sb, :], in0=pt[RS, 0:16], scalar1=0.25, scalar2=None, op0=ALU.mult), reads=[pk], writes=['wiT'])
                            st, sk = nextstage()
                            copy('act', st[RS, 0:64], pt[RS, 16:80], [pk], [sk])
                            dst = kis[l, :, :] if smp else kip[l, tok0 + sb * 128:tok0 + sb * 128 + 128, :]
                            S.dma('sp', dst, st[RS, 0:64], reads=[sk])
                        tm_chunk(wv, wk, 0, 80, sb, ev)
                    pt, pk = nextps()
                    for k in range(16):
                        S.op('pe', lambda e: e.matmul(pt[0:64, 0:N], lhsT=wv[:, k, 16:80], rhs=hT[:, k, 0:N], start=(k == 0), stop=(k == 15)),
                             reads=['hT', wk], writes=[pk])
                    kidst = B.kiTn if smp else B.kiT[:, tok0:tok0 + T]
                    copy(evac_eng(), kidst[0:64, :], pt[0:64, 0:N], [pk], ['kiT'])
                    S.dma('sp', kidst[64:128, :], kidst[0:64, :], reads=['kiT'], writes=['kiT'])

                    def act_evac(dst, func, scale=1.0):
                        return lambda pt, pk: S.op('act', lambda e: e.activation(out=dst, in_=pt[:, 0:N], func=func, scale=scale), reads=[pk], writes=['fmt'])

                    for (c0, C, slot0, func, scale) in ((O_Q, 512, 0, AF.Copy, 128 ** -0.5), (O_Q + 512, 256, 4, AF.Copy, 128 ** -0.5),
                                                         (O_QI, 512, 6, AF.Copy, 1.0), (O_QI + 512, 512, 10, AF.Copy, 1.0),
                                                         (O_GA, 512, 14, AF.Silu, 1.0), (O_GA + 512, 256, 18, AF.Silu, 1.0)):
                        wv, wk = wload(w_in[l], 16, c0, C)
                        for cc in range(C // 128):
                            fm_chunk(wv, wk, cc * 128, 128, act_evac(fmt[:, slot0 + cc, 0:N], func, scale))

                    if smp:
                        attn_sample(B)
                    else:
                        attn_prompt(t, B)

                    if STOPAT < 1.1:
                        return
                    for (c0, C, slot0, func) in ((O_U, 512, 0, AF.Copy), (O_U + 512, 256, 4, AF.Copy), (O_GS, 512, 6, AF.Silu), (O_GS + 512, 256, 10, AF.Silu)):
                        wv, wk = wload(w_in[l], 16, c0, C)
                        for cc in range(C // 128):
                            fm_chunk(wv, wk, cc * 128, 128, act_evac(fmt[:, slot0 + cc, 0:N], func))
                    s5w = NS if smp else 32
                    if STOPAT < 1.13:
                        return
                    with ExitStack() as ph:
                        BU = sbt(ph, [128, s5w + 1, 2, 24], F32)
                        Xb = sbt(ph, [128, 2, 24, s5w], BF16)
                        P1 = sbt(ph, [128, 2, 24], F32); P2 = sbt(ph, [128, 2, 24], F32)
                        zpre = sbt(ph, [128, 6, s5w], F32); gw = sbt(ph, [128, 6, s5w], F32); um = sbt(ph, [128, 4, s5w], BF16)
                        CC = sbt(ph, [128, 24, 2, 128], BF16)
                        S.op('dve', lambda e: e.memset(CC[:], 0.0), writes=['CC'])
                        CCv = CC[:].rearrange("p (k four) r c -> p k four r c", four=4)
                        for a in range(2):
                            ps_ = slice(64 * a, 64 * a + 64)
                            for q_ in range(4):
                                c0_ = 32 * q_ + 16 * a
                                copy('dve', CCv[ps_, :, q_, 0, c0_:c0_ + 16], Cr[ps_, :, :].rearrange("p (k four) c -> p k four c", four=4)[:, :, q_, :], ['Cr'], ['CC'])
                                S.op('dve', lambda e: e.tensor_scalar(out=CCv[ps_, :, q_, 1, c0_:c0_ + 16], in0=Ci[ps_, :, :].rearrange("p (k four) c -> p k four c", four=4)[:, :, q_, :],
                                                                      scalar1=-1.0, scalar2=None, op0=ALU.mult), reads=['Ci'], writes=['CC'])
                        if smp or t == 0:
                            if smp:
                                for ri, srcs in enumerate((sre, sim_)):
                                    for a in range(2):
                                        S.dma('sp', XST[64 * a:64 * a + 64, ri, :], srcs[l].rearrange("(k a) p -> a p k", a=2)[a], writes=['XST'], **NCD)
                            else:
                                S.op('dve', lambda e: e.memset(XST[:], 0.0), writes=['XST'])
                        for sb in range(N // s5w if STOPAT >= 1.15 else 0):
                            cs_ = slice(sb * s5w, (sb + 1) * s5w)
                            copy('dve', BU[:, 0, :, :], XST[:], ['XST'], ['BU'])
                            for ch in range(6):
                                S.op('dve', lambda e: e.tensor_tensor(out=um[:], in0=fmt[:, ch, cs_].unsqueeze(1).broadcast_to([128, 4, s5w]),
                                                                      in1=qm2[:].unsqueeze(2).broadcast_to([128, 4, s5w]), op=ALU.mult),
                                     reads=['fmt', 'qm2'], writes=['um'])
                                for ri in range(2):
                                    pt, pk = nextps()
                                    S.op('pe', lambda e: e.matmul(pt[:, 0:4 * s5w], lhsT=BB[:, ch, ri, :], rhs=um[:].rearrange("p q t -> p (q t)"), start=True, stop=True),
                                         reads=['BB', 'um'], writes=[pk])
                                    if STOPAT < 1.18:
                                        continue
                                    copy(evac_eng(), BU[:, 1:s5w + 1, ri, 4 * ch:4 * ch + 4].rearrange("p t q -> p q t"),
                                         pt[:, 0:4 * s5w].rearrange("p (q t) -> p q t", t=s5w), [pk], ['BU'])
                            if STOPAT < 1.3:
                                continue
                            for tt in range(1, s5w + 1):
                                S.op('dve', lambda e: e.tensor_tensor(out=P1[:], in0=BU[:, tt - 1, :, :], in1=LR2[:], op=ALU.mult), reads=['BU', 'LR2'], writes=['P1'])
                                S.op('dve', lambda e: e.tensor_tensor(out=P2[:], in0=BU[:, tt - 1, :, :], in1=LIs[:], op=ALU.mult), reads=['BU', 'LIs'], writes=['P2'])
                                S.op('dve', lambda e: e.tensor_tensor(out=BU[:, tt, :, :], in0=BU[:, tt, :, :], in1=P1[:], op=ALU.add), reads=['BU', 'P1'], writes=['BU'])
                                S.op('dve', lambda e: e.tensor_tensor(out=BU[:, tt, 0, :], in0=BU[:, tt, 0, :], in1=P2[:, 1, :], op=ALU.add), reads=['BU', 'P2'], writes=['BU'])
                                S.op('dve', lambda e: e.tensor_tensor(out=BU[:, tt, 1, :], in0=BU[:, tt, 1, :], in1=P2[:, 0, :], op=ALU.add), reads=['BU', 'P2'], writes=['BU'])
                            copy('dve', XST[:], BU[:, s5w, :, :], ['BU'], ['XST'])
                            if STOPAT < 1.7:
                                continue
                            for ri in range(2):
                                copy('dve', Xb[:, ri, :, :], BU[:, 1:s5w + 1, ri, :].rearrange("p t q -> p q t"), ['BU'], ['Xb'])
                            for ch in range(6):
                                pt, pk = nextps()
                                n_ = 0
                                for q_ in range(4):
                                    pair = 4 * ch + q_
                                    for ri in range(2):
                                        S.op('pe', lambda e: e.matmul(pt[:, 0:s5w], lhsT=CC[:, pair, ri, :], rhs=Xb[:, ri, pair, :], start=(n_ == 0), stop=(n_ == 7)), reads=['CC', 'Xb'], writes=[pk])
                                        n_ += 1
                                if STOPAT < 1.85:
                                    continue
                                S.op('dve', lambda e: e.tensor_copy(out=gw[:, ch, :], in_=pt[:, 0:s5w]), reads=[pk], writes=['gw'])
                                if STOPAT < 1.87:
                                    continue
                                S.op('dve', lambda e: e.scalar_tensor_tensor(out=zpre[:, ch, :], in0=fmt[:, ch, cs_], scalar=dsk[:, ch:ch + 1], in1=gw[:, ch, :], op0=ALU.mult, op1=ALU.add),
                                     reads=['fmt', 'dsk', 'gw'], writes=['zpre'])
                            if STOPAT < 1.9:
                                continue
                            GC2 = 2.0 * math.sqrt(2.0 / math.pi)
                            S.op('dve', lambda e: e.tensor_tensor(out=gw[:], in0=zpre[:], in1=zpre[:], op=ALU.mult), reads=['zpre'], writes=['gw'])
                            S.op('dve', lambda e: e.tensor_scalar(out=gw[:], in0=gw[:], scalar1=0.044715, scalar2=1.0, op0=ALU.mult, op1=ALU.add), reads=['gw'], writes=['gw'])
                            S.op('dve', lambda e: e.tensor_tensor(out=gw[:], in0=gw[:], in1=zpre[:], op=ALU.mult), reads=['gw', 'zpre'], writes=['gw'])
                            if STOPAT < 1.93:
                                continue
                            S.op('act', lambda e: e.activation(out=gw[:], in_=gw[:], func=AF.Exp, scale=-GC2), reads=['gw'], writes=['gw'])
                            if STOPAT < 1.96:
                                continue
                            S.op('dve', lambda e: e.tensor_scalar(out=gw[:], in0=gw[:], scalar1=1.0, scalar2=None, op0=ALU.add), reads=['gw'], writes=['gw'])
                            S.op('dve', lambda e: e.reciprocal(out=gw[:], in_=gw[:]), reads=['gw'], writes=['gw'])
                            S.op('dve', lambda e: e.tensor_tensor(out=fmt[:, 12:18, cs_], in0=gw[:], in1=zpre[:], op=ALU.mult), reads=['gw', 'zpre'], writes=['fmt'])
                        if STOPAT >= 2 and (smp or t == NTILE - 1):
                            for ri, dst in enumerate((srs, sis) if smp else (srp, sip)):
                                for a in range(2):
                                    S.dma('sp', dst[l].rearrange("(k a) p -> a p k", a=2)[a], XST[64 * a:64 * a + 64, ri, :], reads=['XST'], **NCD)
                        S.barrier()
                    if STOPAT < 3:
                        return
                    with ExitStack() as ph:
                        sgb = sbt(ph, [128, 6, N], BF16); gtmp = sbt(ph, [128, N], F32)
                        for part in (1, 0):
                            wv, wk = wload(w_glu[l], 6, part * 768, 768)
                            for oc in range(6):
                                def ev(pt, pk):
                                    bcol = bgl[:, part * 6 + oc:part * 6 + oc + 1]
                                    if part == 1:
                                        S.op('act', lambda e: e.activation(out=sgb[:, oc, :], in_=pt[:, 0:N], func=AF.Sigmoid, bias=bcol), reads=[pk, 'bgl'], writes=['sgb'])
                                    else:
                                        S.op('dve', lambda e: e.scalar_tensor_tensor(out=gtmp[:], in0=pt[:, 0:N], scalar=bcol, in1=sgb[:, oc, :], op0=ALU.add, op1=ALU.mult),
                                             reads=[pk, 'bgl', 'sgb'], writes=['gtmp'])
                                        S.op('dve', lambda e: e.tensor_tensor(out=xg[:, 6 + oc, 0:N], in0=gtmp[:], in1=fmt[:, 6 + oc, 0:N], op=ALU.mult), reads=['gtmp', 'fmt'], writes=['xg'])
                                fm_chunk(wv, wk, oc * 128, 128, ev, rhsT=fmt, rkey='fmt', nk=6, rsl=12)
                        S.barrier()

                    if STOPAT < 4:
                        return
                    for (c0, C, slot0, func, scale) in ((O_QC, 512, 0, AF.Copy, 128 ** -0.5), (O_GC, 512, 4, AF.Silu, 1.0)):
                        wv, wk = wload(w_in[l], 16, c0, C)
                        for cc in range(4):
                            fm_chunk(wv, wk, cc * 128, 128, act_evac(fmt[:, slot0 + cc, 0:N], func, scale))
                    mkTx, mvx = (B.mkT_s, B.mv_s) if smp else (mkT, mv)
                    mkk, mvk = ('mkT_s', 'mv_s') if smp else ('mkT', 'mv')
                    with ExitStack() as ph:
                        Pc = [sbt(ph, [128, N], BF16) for _ in range(2)]
                        rinv = sbt(ph, [128, N], F32); otmp = sbt(ph, [128, N], F32)
                        for h in range(CR_H):
                            for mbk in range(2):
                                pt, pk = nextps()
                                S.op('pe', lambda e: e.matmul(pt[:, 0:N], lhsT=mkTx[:, h, mbk * 128:(mbk + 1) * 128], rhs=fmt[:, h, 0:N], start=True, stop=True),
                                     reads=[mkk, 'fmt'], writes=[pk])
                                Pm = Pc[mbk]; Pk = f'Pc{mbk}'
                                S.op('act', lambda e: e.activation(out=Pm[:, :], in_=pt[:, 0:N], func=AF.Exp), reads=[pk], writes=[Pk])
                                S.op('pe', lambda e: e.matmul(psa[0][:, 0:N], lhsT=mvx[:, mbk, h * 128:(h + 1) * 128], rhs=Pm[:, :], start=(mbk == 0), stop=(mbk == 1)),
                                     reads=[mvk, Pk], writes=['psa0'])
                                S.op('pe', lambda e: e.matmul(psa[1][:, 0:N], lhsT=ones_b[:], rhs=Pm[:, :], start=(mbk == 0), stop=(mbk == 1)),
                                     reads=['ones_b', Pk], writes=['psa1'])
                            S.op('dve', lambda e: e.reciprocal(out=rinv[:], in_=psa[1][:, 0:N]), reads=['psa1'], writes=['rinv'])
                            S.op('dve', lambda e: e.tensor_tensor(out=otmp[:], in0=psa[0][:, 0:N], in1=rinv[:], op=ALU.mult), reads=['psa0', 'rinv'], writes=['otmp'])
                            S.op('dve', lambda e: e.tensor_tensor(out=xg[:, 12 + h, 0:N], in0=otmp[:], in1=fmt[:, 4 + h, 0:N], op=ALU.mult), reads=['otmp', 'fmt'], writes=['xg'])
                        S.barrier()

                    if STOPAT < 5:
                        return
                    with ExitStack() as ph:
                        gsb = sbt(ph, [128, N], F32); ptmp = sbt(ph, [128, N], F32)
                        for br, (wb_, nkb, xs0) in enumerate(((w_ba, 6, 0), (w_bs, 6, 6), (w_bc, 4, 12))):
                            for grp in range(4):
                                wvg, wkg = wload(w_in[l], 16, O_GM + br * D + grp * 512, 512)
                                wvb, wkb = wload(wb_[l], nkb, grp * 512, 512)
                                for cc in range(4):
                                    dm = grp * 4 + cc
                                    pt, pk = nextps()
                                    for k in range(16):
                                        S.op('pe', lambda e: e.matmul(pt[:, 0:N], lhsT=wvg[:, k, cc * 128:(cc + 1) * 128], rhs=hT[:, k, 0:N], start=(k == 0), stop=(k == 15)),
                                             reads=['hT', wkg], writes=[pk])
                                    S.op('act', lambda e: e.activation(out=gsb[:], in_=pt[:, 0:N], func=AF.Sigmoid), reads=[pk], writes=['gsb'])
                                    pt2, pk2 = nextps()
                                    for k in range(nkb):
                                        S.op('pe', lambda e: e.matmul(pt2[:, 0:N], lhsT=wvb[:, k, cc * 128:(cc + 1) * 128], rhs=xg[:, xs0 + k, 0:N], start=(k == 0), stop=(k == nkb - 1)),
                                             reads=['xg', wkb], writes=[pk2])
                                    if br == 0:
                                        S.op('dve', lambda e: e.tensor_tensor(out=fmt[:, dm, 0:N], in0=pt2[:, 0:N], in1=gsb[:], op=ALU.mult), reads=[pk2, 'gsb'], writes=['fmt'])
                                    else:
                                        S.op('dve', lambda e: e.tensor_tensor(out=ptmp[:], in0=pt2[:, 0:N], in1=gsb[:], op=ALU.mult), reads=[pk2, 'gsb'], writes=['ptmp'])
                                        S.op('dve', lambda e: e.tensor_tensor(out=fmt[:, dm, 0:N], in0=fmt[:, dm, 0:N], in1=ptmp[:], op=ALU.add), reads=['fmt', 'ptmp'], writes=['fmt'])
                        S.barrier()

                    if STOPAT < 6:
                        return
                    with ExitStack() as ph:
                        xres = [sbt(ph, [128, 512], F32) for _ in range(2)]
                        xi = 0
                        for cg in range(4):
                            wv, wk = wload(w_out[l], 16, cg * 512, 512)
                            for sb in range(nsb):
                                pt, pk = nextps()
                                for k in range(16):
                                    S.op('pe', lambda e: e.matmul(pt[RS, 0:512], lhsT=fmt[:, k, sb * sbw:(sb + 1) * sbw], rhs=wv[:, k, :], start=(k == 0), stop=(k == 15)),
                                         reads=['fmt', wk], writes=[pk])
                                r0 = tok0 + sb * sbw
                                xr = xres[xi]; xrk = f'xres{xi}'; xi ^= 1
                                S.dma('sp', xr[RS, :], xsrc[r0:r0 + sbw, cg * 512:(cg + 1) * 512], writes=[xrk])
                                st, sk = nextstage()
                                S.op('dve', lambda e: e.tensor_tensor(out=st[RS, :], in0=pt[RS, 0:512], in1=xr[RS, :], op=ALU.add), reads=[pk, xrk], writes=[sk])
                                S.dma('sp', xdst[r0:r0 + sbw, cg * 512:(cg + 1) * 512], st[RS, :], reads=[sk])
                        S.barrier()

                def attn_prompt(t, B):
                    fmt, xg, wiT, KT, Vc, kiT = B.fmt, B.xg, B.wiT, B.KT, B.Vc, B.kiT
                    with ExitStack() as ph:
                        score = sbt(ph, [128, SEQ], F32); maskb = sbt(ph, [128, SEQ], BF16)
                        rl = [sbt(ph, [128, 512], F32) for _ in range(2)]
                        lo = sbt(ph, [128, 1], F32); mid = sbt(ph, [128, 1], F32); cnt = sbt(ph, [128, 1], F32); btmp = sbt(ph, [128, 1], F32)
                        PT = [sbt(ph, [128, 4, 128], BF16) for _ in range(2)]
                        rinv = sbt(ph, [128, 128], F32); otmp = sbt(ph, [128, 128], F32)
                        pti = [0]
                        for qb in range(4):
                            i = 4 * t + qb; L = (i + 1) * 128
                            qs = slice(qb * 128, (qb + 1) * 128)
                            for kg in range((L + 511) // 512):
                                k0 = kg * 512; n = min(512, L - k0)
                                for h in range(IDX_H):
                                    hf = h % 2; c8 = 6 + h // 2
                                    pt, pk = nextps()
                                    S.op('pe', lambda e: e.matmul(pt[:, 0:n], lhsT=fmt[64 * hf:64 * hf + 64, c8, qs], rhs=kiT[64 * hf:64 * hf + 64, k0:k0 + n], start=True, stop=True),
                                         reads=['fmt', 'kiT'], writes=[pk])
                                    r = rl[h % 2]; rk = f'rl{h % 2}'
                                    S.op('act', lambda e: e.activation(out=r[:, 0:n], in_=pt[:, 0:n], func=AF.Relu), reads=[pk], writes=[rk])
                                    if h == 0:
                                        S.op('dve', lambda e: e.tensor_scalar(out=score[:, k0:k0 + n], in0=r[:, 0:n], scalar1=wiT[:, qb, 0:1], scalar2=None, op0=ALU.mult),
                                             reads=[rk, 'wiT'], writes=['score'])
                                    else:
                                        S.op('dve', lambda e: e.scalar_tensor_tensor(out=score[:, k0:k0 + n], in0=r[:, 0:n], scalar=wiT[:, qb, h:h + 1], in1=score[:, k0:k0 + n], op0=ALU.mult, op1=ALU.add),
                                             reads=[rk, 'wiT', 'score'], writes=['score'])
                            S.op('dve', lambda e: e.tensor_tensor(out=score[:, i * 128:L], in0=score[:, i * 128:L], in1=cmask[:], op=ALU.add), reads=['score', 'cmask'], writes=['score'])
                            S.op('dve', lambda e: e.memset(lo[:], -BIS_R), writes=['lo'])
                            for it in range(BIS_IT):
                                c = BIS_R / (2 ** it)
                                S.op('dve', lambda e: e.tensor_scalar(out=mid[:], in0=lo[:], scalar1=c, scalar2=None, op0=ALU.add), reads=['lo'], writes=['mid'])
                                S.op('dve', lambda e: e.tensor_scalar(out=maskb[:, 0:L], in0=score[:, 0:L], scalar1=mid[:, 0:1], scalar2=0.0, op0=ALU.is_ge, op1=ALU.add, accum_out=cnt[:]),
                                     reads=['score', 'mid'], writes=['maskb', 'cnt'])
                                S.op('dve', lambda e: e.tensor_scalar(out=btmp[:], in0=cnt[:], scalar1=255.5, scalar2=c, op0=ALU.is_ge, op1=ALU.mult), reads=['cnt'], writes=['btmp'])
                                S.op('dve', lambda e: e.tensor_tensor(out=lo[:], in0=lo[:], in1=btmp[:], op=ALU.add), reads=['lo', 'btmp'], writes=['lo'])
                            S.op('dve', lambda e: e.tensor_scalar(out=maskb[:, 0:L], in0=score[:, 0:L], scalar1=lo[:, 0:1], scalar2=NEG, op0=ALU.is_lt, op1=ALU.mult),
                                 reads=['score', 'lo'], writes=['maskb'])
                            for h in range(ATT_H):
                                acc = psa[h % 2]; ak = f'psa{h % 2}'
                                first = True
                                for g0 in range(0, i + 1, 4):
                                    nb = min(4, i + 1 - g0)
                                    pt, pk = nextps()
                                    for jj in range(nb):
                                        j = g0 + jj
                                        S.op('pe', lambda e: e.matmul(pt[:, jj * 128:(jj + 1) * 128], lhsT=KT[:, h, j * 128:(j + 1) * 128], rhs=fmt[:, h, qs], start=True, stop=False),
                                             reads=['KT', 'fmt'], writes=[pk])
                                        S.op('pe', lambda e: e.matmul(pt[:, jj * 128:(jj + 1) * 128], lhsT=maskb[:, j * 128:(j + 1) * 128], rhs=ident_b[:], start=False, stop=False),
                                             reads=['maskb', 'ident_b'], writes=[pk])
                                        S.op('pe', lambda e: e.matmul(pt[:, jj * 128:(jj + 1) * 128], lhsT=biasT[:, h, min(i - j, 2), :], rhs=ident_b[:], start=False, stop=True),
                                             reads=['biasT', 'ident_b'], writes=[pk])
                                    P = PT[pti[0]]; Pk = f'PT{pti[0]}'; pti[0] ^= 1
                                    S.op('act', lambda e: e.activation(out=P[:, 0:nb, :], in_=pt[:, 0:nb * 128].rearrange("p (j q) -> p j q", q=128), func=AF.Exp), reads=[pk], writes=[Pk])
                                    for jj in range(nb):
                                        j = g0 + jj
                                        last = (j == i)
                                        S.op('pe', lambda e: e.matmul(acc[:, 0:128], lhsT=Vc[:, j, h * 128:(h + 1) * 128], rhs=P[:, jj, :], start=first, stop=last), reads=['Vc', Pk], writes=[ak])
                                        first = False
                                        S.op('pe', lambda e: e.matmul(acc[:, 128:256], lhsT=ones_b[:], rhs=P[:, jj, :], start=False, stop=last), reads=['ones_b', Pk], writes=[ak])
                                S.op('dve', lambda e: e.reciprocal(out=rinv[:], in_=acc[:, 128:256]), reads=[ak], writes=['rinv'])
                                S.op('dve', lambda e: e.tensor_tensor(out=otmp[:], in0=acc[:, 0:128], in1=rinv[:], op=ALU.mult), reads=[ak, 'rinv'], writes=['otmp'])
                                S.op('dve', lambda e: e.tensor_tensor(out=xg[:, h, qs], in0=otmp[:], in1=fmt[:, 14 + h, qs], op=ALU.mult), reads=['otmp', 'fmt'], writes=['xg'])
                        S.barrier()

                def attn_sample(B):
                    fmt, xg, wiT, KTs, Vs, kiTn = B.fmt, B.xg, B.wiT, B.KTs, B.Vs, B.kiTn
                    L = PAST + NS
                    R8 = slice(0, NS)
                    with ExitStack() as ph:
                        score = sbt(ph, [128, L], F32); maskb = sbt(ph, [128, L], BF16)
                        rl = [sbt(ph, [128, 512], F32) for _ in range(2)]
                        lo = sbt(ph, [128, 1], F32); mid = sbt(ph, [128, 1], F32); cnt = sbt(ph, [128, 1], F32); btmp = sbt(ph, [128, 1], F32)
                        kdx = [sbt(ph, [128, 2, 64], F32) for _ in range(2)]
                        kiTc = [sbt(ph, [128, 512], BF16) for _ in range(2)]
                        kpg = [sbt(ph, [128, 768], F32) for _ in range(2)]
                        vpg = [sbt(ph, [128, 768], F32) for _ in range(2)]
                        KTp = [sbt(ph, [128, ATT_H, 128], BF16) for _ in range(2)]
                        Vb = [sbt(ph, [128, 768], BF16) for _ in range(2)]
                        Pp = [sbt(ph, [128, 48], BF16) for _ in range(2)]
                        accS = sbt(ph, [128, ATT_H, 16], F32)
                        rinv = sbt(ph, [128, ATT_H, NS], F32); otmp = sbt(ph, [128, ATT_H, NS], F32)

                        def score_chunk(kview, kkey, k0, n):
                            for h in range(IDX_H):
                                hf = h % 2; c8 = 6 + h // 2
                                pt, pk = nextps()
                                S.op('pe', lambda e: e.matmul(pt[R8, 0:n], lhsT=fmt[64 * hf:64 * hf + 64, c8, 0:NS], rhs=kview[64 * hf:64 * hf + 64, 0:n], start=True, stop=True),
                                     reads=['fmt', kkey], writes=[pk])
                                r = rl[h % 2]; rk = f'rl{h % 2}'
                                S.op('act', lambda e: e.activation(out=r[R8, 0:n], in_=pt[R8, 0:n], func=AF.Relu), reads=[pk], writes=[rk])
                                if h == 0:
                                    S.op('dve', lambda e: e.tensor_scalar(out=score[R8, k0:k0 + n], in0=r[R8, 0:n], scalar1=wiT[R8, 0, 0:1], scalar2=None, op0=ALU.mult),
                                         reads=[rk, 'wiT'], writes=['score'])
                                else:
                                    S.op('dve', lambda e: e.scalar_tensor_tensor(out=score[R8, k0:k0 + n], in0=r[R8, 0:n], scalar=wiT[R8, 0, h:h + 1], in1=score[R8, k0:k0 + n], op0=ALU.mult, op1=ALU.add),
                                         reads=[rk, 'wiT', 'score'], writes=['score'])

                        for kg in range(NPAGE // 4):
                            kc = kiTc[kg % 2]; kck = f'kiTc{kg % 2}'
                            pt, pk = nextps()
                            for pj in range(4):
                                j = kg * 4 + pj
                                kd = kdx[j % 2]; kdk = f'kdx{j % 2}'
                                S.idma(kd[:, 0, :], cki[l][:, :], idx_i[:, j:j + 1], reads=['idx_i'], writes=[kdk])
                                copy('dve', kd[:, 1, :], kd[:, 0, :], [kdk], [kdk])
                                S.op('pe', lambda e: e.transpose(pt[:, pj * 128:(pj + 1) * 128], kd[:].rearrange("p a d -> p (a d)"), ident_f[:]),
                                     reads=[kdk, 'ident_f'], writes=[pk])
                            copy('act', kc[:, :], pt[:, 0:512], [pk], [kck])
                            score_chunk(kc, kck, kg * 512, 512)
                        score_chunk(kiTn, 'kiT', PAST, NS)
                        S.op('dve', lambda e: e.tensor_tensor(out=score[R8, PAST:L], in0=score[R8, PAST:L], in1=cmask[R8, 0:NS], op=ALU.add), reads=['score', 'cmask'], writes=['score'])
                        S.op('dve', lambda e: e.memset(lo[:], -BIS_R), writes=['lo'])
                        for it in range(BIS_IT):
                            c = BIS_R / (2 ** it)
                            S.op('dve', lambda e: e.tensor_scalar(out=mid[R8, :], in0=lo[R8, :], scalar1=c, scalar2=None, op0=ALU.add), reads=['lo'], writes=['mid'])
                            S.op('dve', lambda e: e.tensor_scalar(out=maskb[R8, 0:L], in0=score[R8, 0:L], scalar1=mid[R8, 0:1], scalar2=0.0, op0=ALU.is_ge, op1=ALU.add, accum_out=cnt[R8, :]),
                                 reads=['score', 'mid'], writes=['maskb', 'cnt'])
                            S.op('dve', lambda e: e.tensor_scalar(out=btmp[R8, :], in0=cnt[R8, :], scalar1=255.5, scalar2=c, op0=ALU.is_ge, op1=ALU.mult), reads=['cnt'], writes=['btmp'])
                            S.op('dve', lambda e: e.tensor_tensor(out=lo[R8, :], in0=lo[R8, :], in1=btmp[R8, :], op=ALU.add), reads=['lo', 'btmp'], writes=['lo'])
                        S.op('dve', lambda e: e.tensor_scalar(out=maskb[R8, 0:L], in0=score[R8, 0:L], scalar1=lo[R8, 0:1], scalar2=NEG, op0=ALU.is_lt, op1=ALU.mult),
                             reads=['score', 'lo'], writes=['maskb'])

                        S.op('dve', lambda e: e.memset(accS[:], 0.0), writes=['accS'])

                        def accumulate(ptl, pkl, rows, vview, vkey, kT_of, kkey, mcol0, bias_of, Pt, Ptk):
                            nr = rows.stop
                            for h in range(ATT_H):
                                hs = slice(h * NS, (h + 1) * NS)
                                b_ = bias_of(h)
                                S.op('pe', lambda e: e.matmul(ptl[rows, hs], lhsT=kT_of(h), rhs=fmt[:, h, 0:NS], start=True, stop=False), reads=[kkey, 'fmt'], writes=[pkl])
                                S.op('pe', lambda e: e.matmul(ptl[rows, hs], lhsT=maskb[R8, mcol0:mcol0 + nr], rhs=ident_b[R8, R8], start=False, stop=(b_ is None)),
                                     reads=['maskb', 'ident_b'], writes=[pkl])
                                if b_ is not None:
                                    S.op('pe', lambda e: e.matmul(ptl[rows, hs], lhsT=b_, rhs=ident_b[R8, R8], start=False, stop=True), reads=['biasS', 'ident_b'], writes=[pkl])
                            S.op('act', lambda e: e.activation(out=Pt[rows, :], in_=ptl[rows, 0:ATT_H * NS], func=AF.Exp), reads=[pkl], writes=[Ptk])
                            pto, pko = nextps()
                            for h in range(ATT_H):
                                hs = slice(h * NS, (h + 1) * NS)
                                S.op('pe', lambda e: e.matmul(pto[:, h * 16:h * 16 + NS], lhsT=vview[rows, h * 128:(h + 1) * 128], rhs=Pt[rows, hs], start=True, stop=True), reads=[vkey, Ptk], writes=[pko])
                                S.op('pe', lambda e: e.matmul(pto[:, h * 16 + NS:h * 16 + 16], lhsT=ones_b[rows, :], rhs=Pt[rows, hs], start=True, stop=True), reads=['ones_b', Ptk], writes=[pko])
                            S.op('dve', lambda e: e.tensor_tensor(out=accS[:].rearrange("p h c -> p (h c)"), in0=accS[:].rearrange("p h c -> p (h c)"), in1=pto[:, 0:ATT_H * 16], op=ALU.add),
                                 reads=['accS', pko], writes=['accS'])

                        for j in range(NPAGE):
                            b2 = j % 2
                            S.idma(kpg[b2][:, :], ck[l][:, :], idx_i[:, j:j + 1], reads=['idx_i'], writes=[f'kpg{b2}'])
                            S.idma(vpg[b2][:, :], cv[l][:, :], idx_i[:, j:j + 1], reads=['idx_i'], writes=[f'vpg{b2}'])
                            ptA, pkA = nextps()
                            for hh in range(4):
                                S.op('pe', lambda e: e.transpose(ptA[:, hh * 128:(hh + 1) * 128], kpg[b2][:, hh * 128:(hh + 1) * 128], ident_f[:]), reads=[f'kpg{b2}', 'ident_f'], writes=[pkA])
                            copy('act', KTp[b2][:, 0:4, :], ptA[:, 0:512].rearrange("p (h t) -> p h t", t=128), [pkA], [f'KTp{b2}'])
                            ptB, pkB = nextps()
                            for hh in range(2):
                                S.op('pe', lambda e: e.transpose(ptB[:, hh * 128:(hh + 1) * 128], kpg[b2][:, (4 + hh) * 128:(5 + hh) * 128], ident_f[:]), reads=[f'kpg{b2}', 'ident_f'], writes=[pkB])
                            copy('dve', KTp[b2][:, 4:6, :], ptB[:, 0:256].rearrange("p (h t) -> p h t", t=128), [pkB], [f'KTp{b2}'])
                            copy('pool', Vb[b2][:, :], vpg[b2][:, :], [f'vpg{b2}'], [f'Vb{b2}'])
                            ptl, pkl = nextps()
                            accumulate(ptl, pkl, slice(0, 128), Vb[b2], f'Vb{b2}', lambda h: KTp[b2][:, h, :], f'KTp{b2}', j * 128,
                                       (lambda h: biasS[R8, h, 0:128]) if j == NPAGE - 1 else (lambda h: None), Pp[b2], f'Pp{b2}')
                        ptl, pkl = nextps()
                        accumulate(ptl, pkl, R8, Vs, 'Vc', lambda h: KTs[:, h, 0:NS], 'KT', PAST, lambda h: biasS[R8, h, 128:136], Pp[0], 'Pp0')
                        S.op('dve', lambda e: e.reciprocal(out=rinv[:], in_=accS[:, :, NS:16]), reads=['accS'], writes=['rinv'])
                        S.op('dve', lambda e: e.tensor_tensor(out=otmp[:], in0=accS[:, :, 0:NS], in1=rinv[:], op=ALU.mult), reads=['accS', 'rinv'], writes=['otmp'])
                        S.op('dve', lambda e: e.tensor_tensor(out=xg[:, 0:ATT_H, 0:NS], in0=otmp[:], in1=fmt[:, 14:14 + ATT_H, 0:NS], op=ALU.mult), reads=['otmp', 'fmt'], writes=['xg'])
                        S.barrier()

                if SAMPLE:
                    with ExitStack() as sc:
                        B = NS_()
                        B.hT = sbt(sc, [128, 16, NS], BF16); B.fmt = sbt(sc, [128, 20, NS], BF16); B.xg = sbt(sc, [128, 16, NS], BF16)
                        B.wiT = sbt(sc, [128, 1, 16], F32)
                        B.KTs = sbt(sc, [128, ATT_H, NS], BF16); B.Vs = sbt(sc, [128, 768], BF16); B.kiTn = sbt(sc, [128, NS], BF16)
                        mkT_s = sbt(sc, [128, CR_H, MEM], BF16); mv_s = sbt(sc, [128, 2, 512], BF16)
                        with ExitStack() as ph:
                            mt = sbt(ph, [128, 1024], F32)
                            for mbk in range(2):
                                S.dma('sp', mt[:, 0:512], cmk[l, mbk * 128:(mbk + 1) * 128, :], writes=['mt'])
                                S.dma('sp', mt[:, 512:1024], cmv[l, mbk * 128:(mbk + 1) * 128, :], writes=['mt'])
                                copy('act', mv_s[:, mbk, :], mt[:, 512:1024], ['mt'], ['mv_s'])
                                pt, pk = nextps()
                                for h in range(CR_H):
                                    S.op('pe', lambda e: e.transpose(pt[:, h * 128:(h + 1) * 128], mt[:, h * 128:(h + 1) * 128], ident_f[:]),
                                         reads=['mt', 'ident_f'], writes=[pk])
                                copy('dve', mkT_s[:, :, mbk * 128:(mbk + 1) * 128], pt[:, 0:512].rearrange("p (h t) -> p h t", t=128), [pk], ['mkT_s'])
                            S.barrier()
                        B.mkT_s = mkT_s; B.mv_s = mv_s
                        run_tile(True, 0, B)
                        S.barrier()
                if PROMPT:
                    with ExitStack() as sc:
                        B = NS_()
                        B.hT = sbt(sc, [128, 16, T], BF16); B.fmt = sbt(sc, [128, 20, T], BF16); B.xg = sbt(sc, [128, 16, T], BF16)
                        B.wiT = sbt(sc, [128, 4, 16], F32)
                        B.KT = sbt(sc, [128, ATT_H, SEQ], BF16); B.Vc = sbt(sc, [128, 16, 768], BF16); B.kiT = sbt(sc, [128, SEQ], BF16)
                        for t in range(NT):
                            run_tile(False, t, B)
                        S.barrier()
            S.barrier()

        with ExitStack() as ph:
            fngB = sbt(ph, [128, D], F32); xt = sbt(ph, [128, D], F32); junk = sbt(ph, [128, D], BF16); ssq = sbt(ph, [128, 1], F32)
            yo = [sbt(ph, [128, D], F32) for _ in range(2)]
            S.dma('sp', fngB[:], fng.unsqueeze(0).broadcast_to([128, D]), writes=['fngB'])
            jobs = []
            if PROMPT and STOPAT >= 7:
                jobs += [(False, rb) for rb in range(SEQ // 128)]
            if SAMPLE and STOPAT >= 7:
                jobs += [(True, 0)]
            for ji, (smp, rb) in enumerate(jobs):
                RS = slice(0, NS) if smp else slice(0, 128)
                if smp:
                    S.dma('sp', xt[RS, :], xsscr[DEPTH - 1][:, :], writes=['xt'])
                else:
                    S.dma('sp', xt[:], xscr[DEPTH - 1][rb * 128:(rb + 1) * 128, :], writes=['xt'])
                srct = xt; rk = 'xt'
                rms_rstd(srct[RS, :], junk[RS, :], ssq[RS, :], rk)
                y = yo[ji % 2]; yk = f'yo{ji % 2}'
                S.op('dve', lambda e: e.scalar_tensor_tensor(out=y[RS, :], in0=srct[RS, :], scalar=ssq[RS, 0:1], in1=fngB[RS, :], op0=ALU.mult, op1=ALU.mult),
                     reads=[rk, 'ssq', 'fngB'], writes=[yk])
                S.dma('sp', (ys[:, :] if smp else yp[rb * 128:(rb + 1) * 128, :]), y[RS, :], reads=[yk])
            S.barrier()
        S.barrier()
    return nc


_CACHE = {}


def kernel(**inp):
    f32 = lambda a: np.ascontiguousarray(np.asarray(a), dtype=np.float32)
    if 'nc' not in _CACHE:
        _CACHE['nc'] = build_program()
    nc = _CACHE['nc']
    consts = make_consts()
    shared = {
        "norm_g": f32(inp['norm_g']), "w_in": f32(inp['w_in']), "w_ba": f32(inp['w_branch_attn']),
        "w_bs": f32(inp['w_branch_ssm']), "w_bc": f32(inp['w_branch_cross']), "w_out": f32(inp['w_out']),
        "w_mem": f32(inp['w_mem_kv']), "a_re": f32(inp['ssm_a_re']), "a_im": f32(inp['ssm_a_im']),
        "b_re": f32(inp['ssm_b_re']), "b_im": f32(inp['ssm_b_im']), "c_re": f32(inp['ssm_c_re']), "c_im": f32(inp['ssm_c_im']),
        "ssm_d": f32(inp['ssm_d']), "log_dt": f32(inp['ssm_log_dt']), "w_glu": f32(inp['w_glu']), "b_glu": f32(inp['b_glu']),
        "rel_bias": f32(inp['rel_bias']), "fng": f32(inp['final_norm_g']),
    }
    shared.update(consts)
    cache_k = f32(inp['cache_k']); cache_v = f32(inp['cache_v']); cache_kidx = f32(inp['cache_kidx'])
    for l in range(DEPTH if SAMPLE else 0):
        shared[f"ck{l}"] = cache_k[l].reshape(NPOOL * 128, 768)
        shared[f"cv{l}"] = cache_v[l].reshape(NPOOL * 128, 768)
        shared[f"cki{l}"] = cache_kidx[l].reshape(NPOOL * 128, 64)
    xp_all = f32(inp['x_prompt']); xs_all = f32(inp['x_sample']); mem_all = f32(inp['mem_prompt'])
    cmk_all = f32(inp['cache_mem_k']); cmv_all = f32(inp['cache_mem_v'])
    sre_all = f32(inp['state_ssm_re']); sim_all = f32(inp['state_ssm_im'])
    pgt_all = np.ascontiguousarray(np.asarray(inp['page_table']), dtype=np.int32)
    in_maps = []
    for c in range(8):
        m = dict(shared)
        m["xp"] = xp_all[c % 4]; m["xs"] = xs_all[c]; m["memp"] = mem_all[c % 4]
        m["cmk"] = np.ascontiguousarray(cmk_all[:, c]).reshape(DEPTH, MEM, 512)
        m["cmv"] = np.ascontiguousarray(cmv_all[:, c]).reshape(DEPTH, MEM, 512)
        m["sre"] = np.ascontiguousarray(sre_all[:, c]); m["sim"] = np.ascontiguousarray(sim_all[:, c])
        m["pgt"] = pgt_all[c:c + 1]
        in_maps.append(m)
    res = run_bass_kernel_spmd(nc, in_maps, core_ids=list(range(8)))
    R = res.results
    B = 4
    y_prompt = np.stack([R[b]["yp"] for b in range(B)])
    y_sample = np.stack([R[c]["ys"] for c in range(8)])
    st = lambda name, shp: np.stack([R[b][name] for b in range(B)], axis=1).reshape(shp)
    k_prompt = st("kp", (DEPTH, B, SEQ, ATT_H, 128)); v_prompt = st("vp", (DEPTH, B, SEQ, ATT_H, 128))
    kidx_prompt = st("kip", (DEPTH, B, SEQ, 64))
    mk = st("mkp", (DEPTH, B, MEM, CR_H, 128)); mvv = st("mvp", (DEPTH, B, MEM, CR_H, 128))
    sr = st("srp", (DEPTH, B, 48, 64)); si = st("sip", (DEPTH, B, 48, 64))
    ss = lambda name, shp: np.stack([R[c][name] for c in range(8)], axis=1).reshape(shp)
    k_s = ss("ks", (DEPTH, 8, NS, ATT_H, 128)); v_s = ss("vs", (DEPTH, 8, NS, ATT_H, 128)); ki_s = ss("kis", (DEPTH, 8, NS, 64))
    sr_s = ss("srs", (DEPTH, 8, 48, 64)); si_s = ss("sis", (DEPTH, 8, 48, 64))
    return (y_prompt, y_sample, k_prompt, v_prompt, kidx_prompt, mk, mvv, sr, si, k_s, v_s, ki_s, sr_s, si_s)
```

```python
import math
from contextlib import ExitStack
import numpy as np
import concourse.bass as bass
import concourse.mybir as mybir
from concourse.bass_utils import run_bass_kernel_spmd

F32 = mybir.dt.float32; BF16 = mybir.dt.bfloat16; I32 = mybir.dt.int32
AF = mybir.ActivationFunctionType; ALU = mybir.AluOpType; AX = mybir.AxisListType

D = 2048; SEQ = 2048; T = 512; NTILE = 4; DEPTH = 2; NS = 8
ATT_H = 6; IDX_H = 16; IDX_D = 64; CR_H = 4; MEM = 256
PAST = 16384; NPAGE = 128; NPOOL = 1280
O_Q, O_K, O_V, O_GA, O_QI, O_WI, O_KI, O_U, O_GS, O_QC, O_GC, O_GM, O_END = (
    0, 768, 1536, 2304, 3072, 4096, 4112, 4176, 4944, 5712, 6224, 6736, 12880)
EPS = 1e-6
NEG = -30000.0
BIS_R = 512.0
BIS_IT = 24
SEM_LIMIT = 24000

SAMPLE = True
PROMPT = True
STOPAT = 9
VAR = 0
NL = DEPTH
NT = NTILE


class Sched:
    ENG = ['pe', 'act', 'dve', 'pool', 'sp']

    def __init__(self, nc, es, ndma=16):
        self.nc = nc; self.es = es; self.ndma = ndma
        self.eng = {'pe': nc.tensor, 'act': nc.scalar, 'dve': nc.vector, 'pool': nc.gpsimd, 'sp': nc.sync}
        self.epoch = 0
        self.ninst = 0
        self._fresh()

    def _fresh(self):
        es = self.es; nc = self.nc; ep = self.epoch
        self.sem = {e: es.enter_context(nc.semaphore(f"s{ep}_{e}")) for e in self.ENG}
        self.cnt = {e: 0 for e in self.ENG}
        self.dsem = [es.enter_context(nc.semaphore(f"d{ep}_{i}")) for i in range(self.ndma)]
        self.dcnt = [0] * self.ndma
        self.dnext = 0
        self.waited = {e: {} for e in self.ENG}
        self.lastw = {}
        self.readers = {}

    def _wait(self, e, tok):
        skey, val, prod = tok
        if prod == e and e == 'pe':
            return
        if self.waited[e].get(skey, 0) >= val:
            return
        sem = self.sem[skey] if isinstance(skey, str) else self.dsem[skey]
        self.eng[e].wait_ge(sem, val)
        self.waited[e][skey] = val
        self.ninst += 1

    def _deps(self, e, reads, writes):
        for k in reads:
            t = self.lastw.get(k)
            if t is not None:
                self._wait(e, t)
        for k in writes:
            t = self.lastw.get(k)
            if t is not None:
                self._wait(e, t)
            for t in self.readers.get(k, ()):
                self._wait(e, t)

    def _record(self, tok, reads, writes):
        for k in reads:
            self.readers.setdefault(k, []).append(tok)
        for k in writes:
            self.lastw[k] = tok
            self.readers[k] = []

    def op(self, e, fn, reads=(), writes=()):
        writes = list(writes) + [k for k in reads if isinstance(k, str) and k.startswith('ps') and k not in writes]
        self._deps(e, reads, writes)
        inst = fn(self.eng[e])
        self.cnt[e] += 1
        inst.then_inc(self.sem[e], 1)
        self._record((e, self.cnt[e], e), reads, writes)
        self.ninst += 1
        return inst

    def _dma(self, e, mk, reads, writes):
        i = self.dnext
        self.dnext = (self.dnext + 1) % len(self.dsem)
        if self.dcnt[i] > 0:
            self._wait(e, (i, 16 * self.dcnt[i], None))
        self._deps(e, reads, writes)
        inst = mk()
        self.dcnt[i] += 1
        inst.then_inc(self.dsem[i], 16)
        self._record((i, 16 * self.dcnt[i], None), reads, writes)
        self.ninst += 1
        return inst

    def dma(self, e, out, in_, reads=(), writes=(), **kw):
        return self._dma(e, lambda: self.eng[e].dma_start(out=out, in_=in_, **kw), reads, writes)

    def idma(self, out, table, idx_ap, reads=(), writes=()):
        return self._dma('pool', lambda: self.nc.gpsimd.indirect_dma_start(
            out=out, out_offset=None, in_=table, in_offset=bass.IndirectOffsetOnAxis(ap=idx_ap, axis=0)), reads, writes)

    def barrier(self, engs=None):
        for e in (engs or self.ENG):
            for i in range(len(self.dsem)):
                if self.dcnt[i] > 0:
                    self._wait(e, (i, 16 * self.dcnt[i], None))
            for e2 in self.ENG:
                if e2 != e and self.cnt[e2] > 0:
                    self._wait(e, (e2, self.cnt[e2], e2))
        if engs is None and (max(self.cnt.values()) > SEM_LIMIT or 16 * max(self.dcnt) > SEM_LIMIT):
            self.epoch += 1
            self._fresh()


def t5_bucket_np(n):
    n = np.maximum(n, 0)
    max_exact = 16
    nf = np.maximum(n, 1).astype(np.float32)
    large = max_exact + (np.log(nf / max_exact) / math.log(128 / max_exact) * (32 - max_exact)).astype(np.int32)
    large = np.minimum(large, 31)
    return np.where(n < max_exact, n, large)


def make_consts():
    ident = np.eye(128, dtype=np.float32)
    q = np.arange(128)[:, None]; l = np.arange(128)[None, :]
    cmask = np.where(l <= q, 0.0, -1e30).astype(np.float32)
    oh = np.zeros((128, 2, 32, 128), np.float32)
    for dlt in range(2):
        b = t5_bucket_np(q - l + 128 * dlt)
        for bb in range(32):
            oh[:, dlt, bb, :] = (b == bb)
    ohs = np.zeros((NS, 32, 136), np.float32)
    for t in range(NS):
        for off in range(128):
            ohs[t, int(t5_bucket_np(np.array(128 + t - off))), off] = 1.0
        for tk in range(t + 1):
            ohs[t, int(t5_bucket_np(np.array(t - tk))), 128 + tk] = 1.0
    iota = np.arange(128, dtype=np.float32).reshape(128, 1)
    qm = np.zeros((128, 4), np.float32)
    for r in range(128):
        qm[r, r // 32] = 1.0
    return {"c_ident": ident, "c_cmask": cmask, "c_oh": oh.reshape(128, 2 * 32 * 128),
            "c_ohs": ohs.reshape(NS, 32 * 136), "c_iota": iota, "c_qm": qm}


class NS_:
    pass


def build_program():
    nc = bass.Bass("TRN2", target_bir_lowering=False)
    uid = [0]

    def din(name, shape, dt=F32):
        return nc.dram_tensor(name, list(shape), dt, kind="ExternalInput").ap()

    def dout(name, shape, dt=F32):
        return nc.dram_tensor(name, list(shape), dt, kind="ExternalOutput").ap()

    xp = din("xp", [SEQ, D]); xs_in = din("xs", [NS, D])
    memp = din("memp", [MEM, D])
    norm_g = din("norm_g", [DEPTH, D]); w_in = din("w_in", [DEPTH, D, O_END])
    w_ba = din("w_ba", [DEPTH, 768, D]); w_bs = din("w_bs", [DEPTH, 768, D]); w_bc = din("w_bc", [DEPTH, 512, D])
    w_out = din("w_out", [DEPTH, D, D]); w_mem = din("w_mem", [DEPTH, D, 1024])
    a_re = din("a_re", [DEPTH, 48, 64]); a_im = din("a_im", [DEPTH, 48, 64])
    b_re = din("b_re", [DEPTH, 48, 64, 16]); b_im = din("b_im", [DEPTH, 48, 64, 16])
    c_re = din("c_re", [DEPTH, 48, 16, 64]); c_im = din("c_im", [DEPTH, 48, 16, 64])
    ssm_d = din("ssm_d", [DEPTH, 768]); log_dt = din("log_dt", [DEPTH, 48])
    w_glu = din("w_glu", [DEPTH, 768, 1536]); b_glu = din("b_glu", [DEPTH, 1536])
    rel_bias = din("rel_bias", [32, 6]); fng = din("fng", [D])
    c_ident = din("c_ident", [128, 128]); c_cmask = din("c_cmask", [128, 128]); c_oh = din("c_oh", [128, 2 * 32 * 128])
    c_ohs = din("c_ohs", [NS, 32 * 136]); c_iota = din("c_iota", [128, 1]); c_qm = din("c_qm", [128, 4])
    if SAMPLE:
        ck = [din(f"ck{l}", [NPOOL * 128, 768]) for l in range(DEPTH)]
        cv = [din(f"cv{l}", [NPOOL * 128, 768]) for l in range(DEPTH)]
        cki = [din(f"cki{l}", [NPOOL * 128, 64]) for l in range(DEPTH)]
    cmk = din("cmk", [DEPTH, MEM, 512]); cmv = din("cmv", [DEPTH, MEM, 512])
    sre = din("sre", [DEPTH, 48, 64]); sim_ = din("sim", [DEPTH, 48, 64])
    pgt = din("pgt", [1, NPAGE], I32)

    yp = dout("yp", [SEQ, D]); ys = dout("ys", [NS, D])
    kp = dout("kp", [DEPTH, SEQ, 768]); vp = dout("vp", [DEPTH, SEQ, 768]); kip = dout("kip", [DEPTH, SEQ, 64])
    mkp = dout("mkp", [DEPTH, MEM, 512]); mvp = dout("mvp", [DEPTH, MEM, 512])
    srp = dout("srp", [DEPTH, 48, 64]); sip = dout("sip", [DEPTH, 48, 64])
    ks = dout("ks", [DEPTH, NS, 768]); vs = dout("vs", [DEPTH, NS, 768]); kis = dout("kis", [DEPTH, NS, 64])
    srs = dout("srs", [DEPTH, 48, 64]); sis = dout("sis", [DEPTH, 48, 64])
    xscr = nc.dram_tensor("xscr", [2, SEQ, D], F32, kind="Internal").ap()
    xsscr = nc.dram_tensor("xsscr", [2, NS, D], F32, kind="Internal").ap()

    with ExitStack() as es:
        S = Sched(nc, es)
        NCD = dict(allow_slow_non_contiguous=True)

        def sbt(stack, shape, dt, nm="t"):
            uid[0] += 1
            return stack.enter_context(nc.sbuf_tensor(f"{nm}{uid[0]}", list(shape), dt))

        def pst(stack, shape, dt, nm="p"):
            uid[0] += 1
            return stack.enter_context(nc.psum_tensor(f"{nm}{uid[0]}", list(shape), dt))

        ident_f = sbt(es, [128, 128], F32); ident_b = sbt(es, [128, 128], BF16)
        ones_b = sbt(es, [128, 128], BF16); cmask = sbt(es, [128, 128], F32)
        biasT = sbt(es, [128, ATT_H, 3, 128], BF16)
        biasS = sbt(es, [128, ATT_H, 136], BF16)
        idx_i = sbt(es, [128, NPAGE], I32)
        qm2 = sbt(es, [128, 4], F32)
        wbufs = [sbt(es, [128, 8192], BF16, "wb") for _ in range(2)]
        wstate = [0]
        psg = [pst(es, [128, 512], F32) for _ in range(4)]
        psa = [pst(es, [128, 512], F32) for _ in range(2)]
        psb = [pst(es, [128, 1024], BF16) for _ in range(2)]
        psgi = [0]; evi = [0]

        def nextps():
            i = psgi[0]; psgi[0] = (i + 1) % 4
            return psg[i], f"psg{i}"

        def evac_eng():
            evi[0] ^= 1
            return 'act' if evi[0] else 'dve'

        def copy(eng, out, in_, reads, writes):
            if eng == 'act':
                S.op('act', lambda e: e.activation(out=out, in_=in_, func=AF.Copy), reads=reads, writes=writes)
            else:
                S.op(eng, lambda e: e.tensor_copy(out=out, in_=in_), reads=reads, writes=writes)

        def wload(src2d, nk, c0, C):
            i = wstate[0]; wstate[0] ^= 1
            view = wbufs[i][:, 0:nk * C].rearrange("p (k c) -> p k c", c=C)
            S.dma('pool', view, src2d.rearrange("(k p) c -> p k c", p=128)[:, :, c0:c0 + C], writes=[f"wb{i}"])
            return view, f"wb{i}"

        def rms_rstd(src_rows, junk_rows, ssq_rows, rk):
            S.op('act', lambda e: e.activation(out=junk_rows, in_=src_rows, func=AF.Square, accum_out=ssq_rows), reads=[rk], writes=['junk', 'ssq'])
            S.op('dve', lambda e: e.tensor_scalar(out=ssq_rows, in0=ssq_rows, scalar1=1.0 / D, scalar2=EPS, op0=ALU.mult, op1=ALU.add), reads=['ssq'], writes=['ssq'])
            S.op('act', lambda e: e.activation(out=ssq_rows, in_=ssq_rows, func=AF.Sqrt), reads=['ssq'], writes=['ssq'])
            S.op('dve', lambda e: e.reciprocal(out=ssq_rows, in_=ssq_rows), reads=['ssq'], writes=['ssq'])

        S.dma('sp', ident_f[:], c_ident[:, :], writes=['ident_f'])
        S.dma('sp', cmask[:], c_cmask[:, :], writes=['cmask'])
        copy('dve', ident_b[:], ident_f[:], ['ident_f'], ['ident_b'])
        S.op('dve', lambda e: e.memset(ones_b[:], 1.0), writes=['ones_b'])
        S.dma('sp', qm2[:], c_qm[:, :], writes=['qm2'])

        with ExitStack() as ph:
            oh = sbt(ph, [128, 2, 32, 128], BF16); rbB = sbt(ph, [128, 192], F32); acc = sbt(ph, [128, 136], F32)
            S.dma('sp', rbB[:], rel_bias.rearrange("b h -> (b h)").unsqueeze(0).broadcast_to([128, 192]), writes=['rbB'])
            if PROMPT:
                S.dma('pool', oh[:].rearrange("p a (x y) l -> p (a x) (y l)", y=4), c_oh.rearrange("p (x c) -> p x c", c=512), writes=['oh'])
                for h in range(ATT_H):
                    for dlt in range(2):
                        for b in range(32):
                            sc = rbB[:, b * 6 + h:b * 6 + h + 1]
                            if b == 0:
                                S.op('dve', lambda e: e.tensor_scalar(out=acc[:, 0:128], in0=oh[:, dlt, b, :], scalar1=sc, scalar2=None, op0=ALU.mult),
                                     reads=['oh', 'rbB'], writes=['acc'])
                            else:
                                S.op('dve', lambda e: e.scalar_tensor_tensor(out=acc[:, 0:128], in0=oh[:, dlt, b, :], scalar=sc, in1=acc[:, 0:128], op0=ALU.mult, op1=ALU.add),
                                     reads=['oh', 'rbB', 'acc'], writes=['acc'])
                        copy('act', biasT[:, h, dlt, :], acc[:, 0:128], ['acc'], ['biasT'])
                    sc = rbB[:, 31 * 6 + h:31 * 6 + h + 1]
                    S.op('dve', lambda e: e.tensor_scalar(out=biasT[:, h, 2, :], in0=ident_f[:], scalar1=0.0, scalar2=sc, op0=ALU.mult, op1=ALU.add),
                         reads=['ident_f', 'rbB'], writes=['biasT'])
            if SAMPLE:
                ohs = sbt(ph, [128, 32, 136], F32); rbd = sbt(ph, [128, 32, 6], F32)
                pgt_i = sbt(ph, [128, NPAGE], I32); pgt_f = sbt(ph, [128, NPAGE], F32); iota_f = sbt(ph, [128, 1], F32)
                R8 = slice(0, NS)
                S.dma('sp', ohs[R8, :, :], c_ohs.rearrange("q (b k) -> q b k", k=136), writes=['ohs'])
                S.op('dve', lambda e: e.tensor_tensor(out=rbd[R8, :, :], in0=rbB[R8, :].rearrange("p (b h) -> p b h", h=6),
                                                      in1=rbB[R8, 31 * 6:32 * 6].unsqueeze(1).broadcast_to([NS, 32, 6]), op=ALU.subtract),
                     reads=['rbB'], writes=['rbd'])
                for h in range(ATT_H):
                    for b in range(32):
                        sc = rbd[R8, b, h:h + 1]
                        if b == 0:
                            S.op('dve', lambda e: e.tensor_scalar(out=acc[R8, :], in0=ohs[R8, b, :], scalar1=sc, scalar2=None, op0=ALU.mult),
                                 reads=['ohs', 'rbd'], writes=['acc'])
                        else:
                            S.op('dve', lambda e: e.scalar_tensor_tensor(out=acc[R8, :], in0=ohs[R8, b, :], scalar=sc, in1=acc[R8, :], op0=ALU.mult, op1=ALU.add),
                                 reads=['ohs', 'rbd', 'acc'], writes=['acc'])
                    copy('act', biasS[R8, h, :], acc[R8, :], ['acc'], ['biasS'])
                S.dma('sp', pgt_i[:], pgt.rearrange("a n -> (a n)").unsqueeze(0).broadcast_to([128, NPAGE]), writes=['pgt_i'])
                S.dma('sp', iota_f[:], c_iota[:, :], writes=['iota_f'])
                copy('dve', pgt_f[:], pgt_i[:], ['pgt_i'], ['pgt_f'])
                S.op('dve', lambda e: e.tensor_scalar(out=pgt_f[:], in0=pgt_f[:], scalar1=128.0, scalar2=iota_f[:, 0:1], op0=ALU.mult, op1=ALU.add),
                     reads=['pgt_f', 'iota_f'], writes=['pgt_f'])
                copy('dve', idx_i[:], pgt_f[:], ['pgt_f'], ['idx_i'])
            S.barrier()

        for l in range(NL):
            with ExitStack() as ly:
                gT = sbt(ly, [128, 16], F32)
                S.dma('sp', gT[:], norm_g[l].rearrange("(k p) -> p k", p=128), writes=['gT'], **NCD)
                mkT = sbt(ly, [128, CR_H, MEM], BF16); mv = sbt(ly, [128, 2, 512], BF16)
                stage_f = [sbt(ly, [128, 512], F32) for _ in range(2)]
                sti = [0]

                def nextstage():
                    i = sti[0]; sti[0] ^= 1
                    return stage_f[i], f"stage{i}"

                with ExitStack() as ph:
                    memT = sbt(ph, [128, 16, MEM], BF16); mt = sbt(ph, [128, D], F32); mb = sbt(ph, [128, D], BF16)
                    if PROMPT:
                        for mbk in range(2):
                            S.dma('sp', mt[:], memp[mbk * 128:(mbk + 1) * 128, :], writes=['mt'])
                            copy('act', mb[:], mt[:], ['mt'], ['mb'])
                            for half in range(2):
                                pb = psb[half]
                                for k8 in range(8):
                                    k = half * 8 + k8
                                    S.op('pe', lambda e: e.transpose(pb[:, k8 * 128:(k8 + 1) * 128], mb[:, k * 128:(k + 1) * 128], ident_b[:]),
                                         reads=['mb', 'ident_b'], writes=[f'psb{half}'])
                                copy(evac_eng(), memT[:, half * 8:half * 8 + 8, mbk * 128:(mbk + 1) * 128],
                                     pb[:].rearrange("p (k t) -> p k t", t=128), [f'psb{half}'], ['memT'])
                        for part in range(2):
                            wv, wk = wload(w_mem[l], 16, part * 512, 512)
                            for mbk in range(2):
                                pt, pk = nextps()
                                for k in range(16):
                                    S.op('pe', lambda e: e.matmul(pt[:, :], lhsT=memT[:, k, mbk * 128:(mbk + 1) * 128], rhs=wv[:, k, :], start=(k == 0), stop=(k == 15)),
                                         reads=['memT', wk], writes=[pk])
                                st, sk = nextstage()
                                copy(evac_eng(), st[:], pt[:, :], [pk], [sk])
                                S.dma('sp', (mkp if part == 0 else mvp)[l, mbk * 128:(mbk + 1) * 128, :], st[:], reads=[sk])
                                if part == 1:
                                    copy(evac_eng(), mv[:, mbk, :], pt[:, :], [pk], ['mv'])
                            if part == 0:
                                for h in range(CR_H):
                                    pt, pk = nextps()
                                    for k in range(16):
                                        S.op('pe', lambda e: e.matmul(pt[:, 0:MEM], lhsT=wv[:, k, h * 128:(h + 1) * 128], rhs=memT[:, k, :], start=(k == 0), stop=(k == 15)),
                                             reads=['memT', wk], writes=[pk])
                                    copy(evac_eng(), mkT[:, h, :], pt[:, 0:MEM], [pk], ['mkT'])
                    S.barrier()

                lr = sbt(ly, [128, 24], F32); li = sbt(ly, [128, 24], F32)
                LL = sbt(ly, [128, 2, 2, 24], F32)
                LR2 = LL[:, 0, :, :]; LIs = LL[:, 1, :, :]
                BB = sbt(ly, [128, 6, 2, 128], BF16)
                Cr = sbt(ly, [128, 24, 16], F32); Ci = sbt(ly, [128, 24, 16], F32)
                dsk = sbt(ly, [128, 6], F32); bgl = sbt(ly, [128, 12], F32)
                XST = sbt(ly, [128, 2, 24], F32)
                with ExitStack() as ph:
                    lamr = sbt(ph, [128, 24], F32); lami = sbt(ph, [128, 24], F32); stp = sbt(ph, [128, 24], F32)
                    t1 = sbt(ph, [128, 24], F32); t2 = sbt(ph, [128, 24], F32); t3 = sbt(ph, [128, 24], F32)
                    cs = sbt(ph, [128, 24], F32); sn = sbt(ph, [128, 24], F32); mag = sbt(ph, [128, 24], F32)
                    cr = sbt(ph, [128, 24], F32); ci = sbt(ph, [128, 24], F32)
                    Br = sbt(ph, [128, 24, 16], F32); Bi = sbt(ph, [128, 24, 16], F32)
                    Bbr = sbt(ph, [128, 24, 16], F32); Bbi = sbt(ph, [128, 24, 16], F32); Btmp = sbt(ph, [128, 24, 16], F32)
                    Bblk = [sbt(ph, [128, 24, 2, 16], F32) for _ in range(2)]
                    halfpi = sbt(ph, [128, 1], F32)
                    S.op('dve', lambda e: e.memset(halfpi[:], math.pi / 2), writes=['halfpi'])
                    for a in range(2):
                        ps_ = slice(64 * a, 64 * a + 64)
                        S.dma('sp', lamr[ps_, :], a_re[l].rearrange("(k a) p -> a p k", a=2)[a], writes=['lamr'], **NCD)
                        S.dma('sp', lami[ps_, :], a_im[l].rearrange("(k a) p -> a p k", a=2)[a], writes=['lami'], **NCD)
                        S.dma('sp', stp[ps_, :], log_dt[l].rearrange("(k a) -> a k", a=2)[a:a + 1, :].broadcast_to([64, 24]), writes=['stp'], **NCD)
                        S.dma('sp', Br[ps_, :, :], b_re[l].rearrange("(k a) p c -> a p k c", a=2)[a], writes=['Br'], **NCD)
                        S.dma('sp', Bi[ps_, :, :], b_im[l].rearrange("(k a) p c -> a p k c", a=2)[a], writes=['Bi'], **NCD)
                        for c_ in range(16):
                            S.dma('sp', Cr[ps_, :, c_], c_re[l].rearrange("(k a) c p -> a c p k", a=2)[a, c_], writes=['Cr'], **NCD)
                            S.dma('sp', Ci[ps_, :, c_], c_im[l].rearrange("(k a) c p -> a c p k", a=2)[a, c_], writes=['Ci'], **NCD)
                    S.dma('sp', dsk[:], ssm_d[l].rearrange("(k p) -> p k", p=128), writes=['dsk'], **NCD)
                    S.dma('sp', bgl[:], b_glu[l].rearrange("(k p) -> p k", p=128), writes=['bgl'], **NCD)

                    def V(out, in0, in1, op, rd, wr):
                        S.op('dve', lambda e: e.tensor_tensor(out=out, in0=in0, in1=in1, op=op), reads=rd, writes=wr)

                    S.op('act', lambda e: e.activation(out=stp[:], in_=stp[:], func=AF.Exp), reads=['stp'], writes=['stp'])
                    S.op('dve', lambda e: e.tensor_scalar(out=lamr[:], in0=lamr[:], scalar1=-1e-4, scalar2=None, op0=ALU.min), reads=['lamr'], writes=['lamr'])
                    V(t1[:], lamr[:], stp[:], ALU.mult, ['lamr', 'stp'], ['t1'])
                    S.op('act', lambda e: e.activation(out=mag[:], in_=t1[:], func=AF.Exp), reads=['t1'], writes=['mag'])
                    V(t2[:], lami[:], stp[:], ALU.mult, ['lami', 'stp'], ['t2'])
                    S.op('act', lambda e: e.activation(out=sn[:], in_=t2[:], func=AF.Sin, scale=1.0 / 16), reads=['t2'], writes=['sn'])
                    S.op('act', lambda e: e.activation(out=cs[:], in_=t2[:], func=AF.Sin, scale=1.0 / 16, bias=halfpi[:]), reads=['t2', 'halfpi'], writes=['cs'])
                    for _ in range(4):
                        V(t1[:], cs[:], cs[:], ALU.mult, ['cs'], ['t1'])
                        V(t3[:], sn[:], sn[:], ALU.mult, ['sn'], ['t3'])
                        V(sn[:], cs[:], sn[:], ALU.mult, ['cs', 'sn'], ['sn'])
                        S.op('dve', lambda e: e.tensor_scalar(out=sn[:], in0=sn[:], scalar1=2.0, scalar2=None, op0=ALU.mult), reads=['sn'], writes=['sn'])
                        V(cs[:], t1[:], t3[:], ALU.subtract, ['t1', 't3'], ['cs'])
                    V(lr[:], mag[:], cs[:], ALU.mult, ['mag', 'cs'], ['lr'])
                    V(li[:], mag[:], sn[:], ALU.mult, ['mag', 'sn'], ['li'])
                    S.op('dve', lambda e: e.tensor_scalar(out=t1[:], in0=lr[:], scalar1=-1.0, scalar2=None, op0=ALU.add), reads=['lr'], writes=['t1'])
                    V(t2[:], lamr[:], lamr[:], ALU.mult, ['lamr'], ['t2'])
                    V(t3[:], lami[:], lami[:], ALU.mult, ['lami'], ['t3'])
                    V(t2[:], t2[:], t3[:], ALU.add, ['t2', 't3'], ['t2'])
                    S.op('dve', lambda e: e.reciprocal(out=t2[:], in_=t2[:]), reads=['t2'], writes=['t2'])
                    V(cr[:], t1[:], lamr[:], ALU.mult, ['t1', 'lamr'], ['cr'])
                    V(t3[:], li[:], lami[:], ALU.mult, ['li', 'lami'], ['t3'])
                    V(cr[:], cr[:], t3[:], ALU.add, ['cr', 't3'], ['cr'])
                    V(cr[:], cr[:], t2[:], ALU.mult, ['cr', 't2'], ['cr'])
                    V(ci[:], li[:], lamr[:], ALU.mult, ['li', 'lamr'], ['ci'])
                    V(t3[:], t1[:], lami[:], ALU.mult, ['t1', 'lami'], ['t3'])
                    V(ci[:], ci[:], t3[:], ALU.subtract, ['ci', 't3'], ['ci'])
                    V(ci[:], ci[:], t2[:], ALU.mult, ['ci', 't2'], ['ci'])
                    crb = cr[:].unsqueeze(2).broadcast_to([128, 24, 16]); cib = ci[:].unsqueeze(2).broadcast_to([128, 24, 16])
                    V(Bbr[:], Br[:], crb, ALU.mult, ['Br', 'cr'], ['Bbr'])
                    V(Btmp[:], Bi[:], cib, ALU.mult, ['Bi', 'ci'], ['Btmp'])
                    V(Bbr[:], Bbr[:], Btmp[:], ALU.subtract, ['Bbr', 'Btmp'], ['Bbr'])
                    V(Bbi[:], Bi[:], crb, ALU.mult, ['Bi', 'cr'], ['Bbi'])
                    V(Btmp[:], Br[:], cib, ALU.mult, ['Br', 'ci'], ['Btmp'])
                    V(Bbi[:], Bbi[:], Btmp[:], ALU.add, ['Bbi', 'Btmp'], ['Bbi'])
                    for ri, src in enumerate((Bbr, Bbi)):
                        S.op('dve', lambda e: e.memset(Bblk[ri][:], 0.0), writes=[f'Bblk{ri}'])
                        for a in range(2):
                            ps_ = slice(64 * a, 64 * a + 64)
                            copy('dve', Bblk[ri][ps_, :, a, :], src[ps_, :, :], [f'Bb{"ri"[ri]}'], [f'Bblk{ri}'])
                        for ch in range(6):
                            pt, pk = nextps()
                            S.op('pe', lambda e: e.transpose(pt[:, 0:128], Bblk[ri][:, 4 * ch:4 * ch + 4, :, :].rearrange("p q a c -> p (q a c)"), ident_f[:]),
                                 reads=[f'Bblk{ri}', 'ident_f'], writes=[pk])
                            copy(evac_eng(), BB[:, ch, ri, :], pt[:, 0:128], [pk], ['BB'])
                    for r_ in range(2):
                        copy('dve', LR2[:, r_, :], lr[:], ['lr'], ['LR2'])
                    copy('dve', LIs[:, 0, :], li[:], ['li'], ['LIs'])
                    S.op('dve', lambda e: e.tensor_scalar(out=LIs[:, 1, :], in0=li[:], scalar1=-1.0, scalar2=None, op0=ALU.mult), reads=['li'], writes=['LIs'])
                    S.barrier()

                def run_tile(smp, t, B):
                    N = NS if smp else T
                    sbw = NS if smp else 128
                    nsb = 1 if smp else 4
                    tok0 = 0 if smp else t * T
                    RS = slice(0, sbw)
                    hT, fmt, xg, wiT = B.hT, B.fmt, B.xg, B.wiT
                    if smp:
                        xsrc = xs_in if l == 0 else xsscr[0]
                        xdst = xsscr[l]
                    else:
                        xsrc = xp if l == 0 else xscr[0]
                        xdst = xscr[l]

                    with ExitStack() as ph:
                        xt = sbt(ph, [128, D], F32); xb = sbt(ph, [128, D], BF16); junk = sbt(ph, [128, D], BF16)
                        ssq = sbt(ph, [128, 1], F32)
                        for sb in range(nsb):
                            r0 = tok0 + sb * sbw
                            S.dma('sp', xt[RS, :], xsrc[r0:r0 + sbw, :], writes=['xt'])
                            srct = xt; rk = 'xt'
                            rms_rstd(srct[RS, :], junk[RS, :], ssq[RS, :], rk)
                            S.op('act', lambda e: e.activation(out=xb[RS, :], in_=srct[RS, :], func=AF.Copy, scale=ssq[RS, :]), reads=[rk, 'ssq'], writes=['xb'])
                            for half in range(2):
                                pb = psb[half]
                                for k8 in range(8):
                                    k = half * 8 + k8
                                    S.op('pe', lambda e: e.transpose(pb[:, k8 * sbw:(k8 + 1) * sbw], xb[RS, k * 128:(k + 1) * 128], ident_b[RS, RS]),
                                         reads=['xb', 'ident_b'], writes=[f'psb{half}'])
                                S.op('dve', lambda e: e.tensor_tensor(out=hT[:, half * 8:half * 8 + 8, sb * sbw:(sb + 1) * sbw],
                                                                      in0=pb[:, 0:8 * sbw].rearrange("p (k t) -> p k t", t=sbw),
                                                                      in1=gT[:, half * 8:half * 8 + 8].unsqueeze(2).broadcast_to([128, 8, sbw]), op=ALU.mult),
                                     reads=[f'psb{half}', 'gT'], writes=['hT'])
                        S.barrier()

                    def fm_chunk(wv, wk, cl, ncols, evac, rhsT=hT, rkey='hT', nk=16, rsl=None):
                        pt, pk = nextps()
                        for k in range(nk):
                            rr = rhsT[:, k, 0:N] if rsl is None else rhsT[:, rsl + k, 0:N]
                            S.op('pe', lambda e: e.matmul(pt[0:ncols, 0:N], lhsT=wv[:, k, cl:cl + ncols], rhs=rr, start=(k == 0), stop=(k == nk - 1)),
                                 reads=[rkey, wk], writes=[pk])
                        evac(pt, pk)

                    def tm_chunk(wv, wk, cl, ncols, sb, evac):
                        pt, pk = nextps()
                        for k in range(16):
                            S.op('pe', lambda e: e.matmul(pt[RS, 0:ncols], lhsT=hT[:, k, sb * sbw:(sb + 1) * sbw], rhs=wv[:, k, cl:cl + ncols], start=(k == 0), stop=(k == 15)),
                                 reads=['hT', wk], writes=[pk])
                        evac(pt, pk)

                    for (c0, C) in ((O_K, 512), (O_K + 512, 256)):
                        wv, wk = wload(w_in[l], 16, c0, C)
                        for cc in range(C // 128):
                            h = (c0 - O_K) // 128 + cc
                            kdst = B.KTs[:, h, 0:NS] if smp else B.KT[:, h, tok0:tok0 + T]
                            fm_chunk(wv, wk, cc * 128, 128,
                                     lambda pt, pk: copy(evac_eng(), kdst, pt[:, 0:N], [pk], ['KT']))
                        for sb in range(nsb):
                            def ev(pt, pk):
                                st, sk = nextstage()
                                copy(evac_eng(), st[RS, 0:C], pt[RS, 0:C], [pk], [sk])
                                dst = ks[l, :, c0 - O_K:c0 - O_K + C] if smp else kp[l, tok0 + sb * 128:tok0 + sb * 128 + 128, c0 - O_K:c0 - O_K + C]
                                S.dma('sp', dst, st[RS, 0:C], reads=[sk])
                            tm_chunk(wv, wk, 0, C, sb, ev)
                    for (c0, C) in ((O_V, 512), (O_V + 512, 256)):
                        wv, wk = wload(w_in[l], 16, c0, C)
                        for sb in range(nsb):
                            def ev(pt, pk):
                                st, sk = nextstage()
                                copy('act', st[RS, 0:C], pt[RS, 0:C], [pk], [sk])
                                dst = vs[l, :, c0 - O_V:c0 - O_V + C] if smp else vp[l, tok0 + sb * 128:tok0 + sb * 128 + 128, c0 - O_V:c0 - O_V + C]
                                S.dma('sp', dst, st[RS, 0:C], reads=[sk])
                                vdst = B.Vs[RS, c0 - O_V:c0 - O_V + C] if smp else B.Vc[:, t * 4 + sb, c0 - O_V:c0 - O_V + C]
                                copy('dve', vdst, pt[RS, 0:C], [pk], ['Vc'])
                            tm_chunk(wv, wk, 0, C, sb, ev)
                    wv, wk = wload(w_in[l], 16, O_WI, 80)
                    for sb in range(nsb):
                        def ev(pt, pk):
                            S.op('dve', lambda e: e.tensor_scalar(out=wiT[RS, sb, :], in0=pt[RS, 0:16], scalar1=0.25, scalar2=None, op0=ALU.mult), reads=[pk], writes=['wiT'])
                            st, sk = nextstage()
                            copy('act', st[RS, 0:64], pt[RS, 16:80], [pk], [sk])
                            dst = kis[l, :, :] if smp else kip[l, tok0 + sb * 128:tok0 + sb * 128 + 128, :]
                            S.dma('sp', dst, st[RS, 0:64], reads=[sk])
                        tm_chunk(wv, wk, 0, 80, sb, ev)
                    pt, pk = nextps()
                    for k in range(16):
                        S.op('pe', lambda e: e.matmul(pt[0:64, 0:N], lhsT=wv[:, k, 16:80], rhs=hT[:, k, 0:N], start=(k == 0), stop=(k == 15)),
                             reads=['hT', wk], writes=[pk])
                    kidst = B.kiTn if smp else B.kiT[:, tok0:tok0 + T]
                    copy(evac_eng(), kidst[0:64, :], pt[0:64, 0:N], [pk], ['kiT'])
                    S.dma('sp', kidst[64:128, :], kidst[0:64, :], reads=['kiT'], writes=['kiT'])

                    def act_evac(dst, func, scale=1.0):
                        return lambda pt, pk: S.op('act', lambda e: e.activation(out=dst, in_=pt[:, 0:N], func=func, scale=scale), reads=[pk], writes=['fmt'])

                    for (c0, C, slot0, func, scale) in ((O_Q, 512, 0, AF.Copy, 128 ** -0.5), (O_Q + 512, 256, 4, AF.Copy, 128 ** -0.5),
                                                         (O_QI, 512, 6, AF.Copy, 1.0), (O_QI + 512, 512, 10, AF.Copy, 1.0),
                                                         (O_GA, 512, 14, AF.Silu, 1.0), (O_GA + 512, 256, 18, AF.Silu, 1.0)):
                        wv, wk = wload(w_in[l], 16, c0, C)
                        for cc in range(C // 128):
                            fm_chunk(wv, wk, cc * 128, 128, act_evac(fmt[:, slot0 + cc, 0:N], func, scale))

                    if smp:
                        attn_sample(B)
                    else:
                        attn_prompt(t, B)

                    if STOPAT < 1.1:
                        return
                    for (c0, C, slot0, func) in ((O_U, 512, 0, AF.Copy), (O_U + 512, 256, 4, AF.Copy), (O_GS, 512, 6, AF.Silu), (O_GS + 512, 256, 10, AF.Silu)):
                        wv, wk = wload(w_in[l], 16, c0, C)
                        for cc in range(C // 128):
                            fm_chunk(wv, wk, cc * 128, 128, act_evac(fmt[:, slot0 + cc, 0:N], func))
                    s5w = NS if smp else 32
                    if STOPAT < 1.13:
                        return
                    with ExitStack() as ph:
                        BU = sbt(ph, [128, s5w + 1, 2, 24], F32)
                        Xb = sbt(ph, [128, 2, 24, s5w], BF16)
                        P12 = sbt(ph, [128, 2, 2, 24], F32)
                        zpre = sbt(ph, [128, 6, s5w], F32); gw = sbt(ph, [128, 6, s5w], F32); um = sbt(ph, [128, 4, s5w], BF16)
                        CC = sbt(ph, [128, 24, 2, 128], BF16)
                        S.op('dve', lambda e: e.memset(CC[:], 0.0), writes=['CC'])
                        CCv = CC[:].rearrange("p (k four) r c -> p k four r c", four=4)
                        for a in range(2):
                            ps_ = slice(64 * a, 64 * a + 64)
                            for q_ in range(4):
                                c0_ = 32 * q_ + 16 * a
                                copy('dve', CCv[ps_, :, q_, 0, c0_:c0_ + 16], Cr[ps_, :, :].rearrange("p (k four) c -> p k four c", four=4)[:, :, q_, :], ['Cr'], ['CC'])
                                S.op('dve', lambda e: e.tensor_scalar(out=CCv[ps_, :, q_, 1, c0_:c0_ + 16], in0=Ci[ps_, :, :].rearrange("p (k four) c -> p k four c", four=4)[:, :, q_, :],
                                                                      scalar1=-1.0, scalar2=None, op0=ALU.mult), reads=['Ci'], writes=['CC'])
                        if smp or t == 0:
                            if smp:
                                for ri, srcs in enumerate((sre, sim_)):
                                    for a in range(2):
                                        S.dma('sp', XST[64 * a:64 * a + 64, ri, :], srcs[l].rearrange("(k a) p -> a p k", a=2)[a], writes=['XST'], **NCD)
                            else:
                                S.op('dve', lambda e: e.memset(XST[:], 0.0), writes=['XST'])
                        for sb in range(N // s5w if STOPAT >= 1.15 else 0):
                            cs_ = slice(sb * s5w, (sb + 1) * s5w)
                            copy('dve', BU[:, 0, :, :], XST[:], ['XST'], ['BU'])
                            for ch in range(6):
                                S.op('dve', lambda e: e.tensor_tensor(out=um[:], in0=fmt[:, ch, cs_].unsqueeze(1).broadcast_to([128, 4, s5w]),
                                                                      in1=qm2[:].unsqueeze(2).broadcast_to([128, 4, s5w]), op=ALU.mult),
                                     reads=['fmt', 'qm2'], writes=['um'])
                                for ri in range(2):
                                    pt, pk = nextps()
                                    S.op('pe', lambda e: e.matmul(pt[:, 0:4 * s5w], lhsT=BB[:, ch, ri, :], rhs=um[:].rearrange("p q t -> p (q t)"), start=True, stop=True),
                                         reads=['BB', 'um'], writes=[pk])
                                    if STOPAT < 1.18:
                                        continue
                                    copy(evac_eng(), BU[:, 1:s5w + 1, ri, 4 * ch:4 * ch + 4].rearrange("p t q -> p q t"),
                                         pt[:, 0:4 * s5w].rearrange("p (q t) -> p q t", t=s5w), [pk], ['BU'])
                            if STOPAT < 1.3:
                                continue
                            for tt in range(1, s5w + 1):
                                S.op('dve', lambda e: e.tensor_tensor(out=P12[:], in0=BU[:, tt - 1, :, :].unsqueeze(1).broadcast_to([128, 2, 2, 24]), in1=LL[:], op=ALU.mult),
                                     reads=['BU', 'LR2', 'LIs'], writes=['P12'])
                                S.op('dve', lambda e: e.tensor_tensor(out=BU[:, tt, :, :], in0=BU[:, tt, :, :], in1=P12[:, 0, :, :], op=ALU.add), reads=['BU', 'P12'], writes=['BU'])
                                S.op('dve', lambda e: e.tensor_tensor(out=BU[:, tt, 0, :], in0=BU[:, tt, 0, :], in1=P12[:, 1, 1, :], op=ALU.add), reads=['BU', 'P12'], writes=['BU'])
                                S.op('dve', lambda e: e.tensor_tensor(out=BU[:, tt, 1, :], in0=BU[:, tt, 1, :], in1=P12[:, 1, 0, :], op=ALU.add), reads=['BU', 'P12'], writes=['BU'])
                            copy('dve', XST[:], BU[:, s5w, :, :], ['BU'], ['XST'])
                            if STOPAT < 1.7:
                                continue
                            for ri in range(2):
                                copy('dve', Xb[:, ri, :, :], BU[:, 1:s5w + 1, ri, :].rearrange("p t q -> p q t"), ['BU'], ['Xb'])
                            for ch in range(6):
                                pt, pk = nextps()
                                n_ = 0
                                for q_ in range(4):
                                    pair = 4 * ch + q_
                                    for ri in range(2):
                                        S.op('pe', lambda e: e.matmul(pt[:, 0:s5w], lhsT=CC[:, pair, ri, :], rhs=Xb[:, ri, pair, :], start=(n_ == 0), stop=(n_ == 7)), reads=['CC', 'Xb'], writes=[pk])
                                        n_ += 1
                                if STOPAT < 1.85:
                                    continue
                                S.op('dve', lambda e: e.tensor_copy(out=gw[:, ch, :], in_=pt[:, 0:s5w]), reads=[pk], writes=['gw'])
                                if STOPAT < 1.87:
                                    continue
                                S.op('dve', lambda e: e.scalar_tensor_tensor(out=zpre[:, ch, :], in0=fmt[:, ch, cs_], scalar=dsk[:, ch:ch + 1], in1=gw[:, ch, :], op0=ALU.mult, op1=ALU.add),
                                     reads=['fmt', 'dsk', 'gw'], writes=['zpre'])
                            if STOPAT < 1.9:
                                continue
                            GC2 = 2.0 * math.sqrt(2.0 / math.pi)
                            S.op('dve', lambda e: e.tensor_tensor(out=gw[:], in0=zpre[:], in1=zpre[:], op=ALU.mult), reads=['zpre'], writes=['gw'])
                            S.op('dve', lambda e: e.tensor_scalar(out=gw[:], in0=gw[:], scalar1=0.044715, scalar2=1.0, op0=ALU.mult, op1=ALU.add), reads=['gw'], writes=['gw'])
                            S.op('dve', lambda e: e.tensor_tensor(out=gw[:], in0=gw[:], in1=zpre[:], op=ALU.mult), reads=['gw', 'zpre'], writes=['gw'])
                            if STOPAT < 1.93:
                                continue
                            S.op('act', lambda e: e.activation(out=gw[:], in_=gw[:], func=AF.Exp, scale=-GC2), reads=['gw'], writes=['gw'])
                            if STOPAT < 1.96:
                                continue
                            S.op('dve', lambda e: e.tensor_scalar(out=gw[:], in0=gw[:], scalar1=1.0, scalar2=None, op0=ALU.add), reads=['gw'], writes=['gw'])
                            S.op('dve', lambda e: e.reciprocal(out=gw[:], in_=gw[:]), reads=['gw'], writes=['gw'])
                            S.op('dve', lambda e: e.tensor_tensor(out=fmt[:, 12:18, cs_], in0=gw[:], in1=zpre[:], op=ALU.mult), reads=['gw', 'zpre'], writes=['fmt'])
                        if STOPAT >= 2 and (smp or t == NTILE - 1):
                            for ri, dst in enumerate((srs, sis) if smp else (srp, sip)):
                                for a in range(2):
                                    S.dma('sp', dst[l].rearrange("(k a) p -> a p k", a=2)[a], XST[64 * a:64 * a + 64, ri, :], reads=['XST'], **NCD)
                        S.barrier()
                    if STOPAT < 3:
                        return
                    with ExitStack() as ph:
                        sgb = sbt(ph, [128, 6, N], BF16); gtmp = sbt(ph, [128, N], F32)
                        for part in (1, 0):
                            wv, wk = wload(w_glu[l], 6, part * 768, 768)
                            for oc in range(6):
                                def ev(pt, pk):
                                    bcol = bgl[:, part * 6 + oc:part * 6 + oc + 1]
                                    if part == 1:
                                        S.op('act', lambda e: e.activation(out=sgb[:, oc, :], in_=pt[:, 0:N], func=AF.Sigmoid, bias=bcol), reads=[pk, 'bgl'], writes=['sgb'])
                                    else:
                                        S.op('dve', lambda e: e.scalar_tensor_tensor(out=gtmp[:], in0=pt[:, 0:N], scalar=bcol, in1=sgb[:, oc, :], op0=ALU.add, op1=ALU.mult),
                                             reads=[pk, 'bgl', 'sgb'], writes=['gtmp'])
                                        S.op('dve', lambda e: e.tensor_tensor(out=xg[:, 6 + oc, 0:N], in0=gtmp[:], in1=fmt[:, 6 + oc, 0:N], op=ALU.mult), reads=['gtmp', 'fmt'], writes=['xg'])
                                fm_chunk(wv, wk, oc * 128, 128, ev, rhsT=fmt, rkey='fmt', nk=6, rsl=12)
                        S.barrier()

                    if STOPAT < 4:
                        return
                    for (c0, C, slot0, func, scale) in ((O_QC, 512, 0, AF.Copy, 128 ** -0.5), (O_GC, 512, 4, AF.Silu, 1.0)):
                        wv, wk = wload(w_in[l], 16, c0, C)
                        for cc in range(4):
                            fm_chunk(wv, wk, cc * 128, 128, act_evac(fmt[:, slot0 + cc, 0:N], func, scale))
                    mkTx, mvx = (B.mkT_s, B.mv_s) if smp else (mkT, mv)
                    mkk, mvk = ('mkT_s', 'mv_s') if smp else ('mkT', 'mv')
                    with ExitStack() as ph:
                        Pc = [sbt(ph, [128, N], BF16) for _ in range(2)]
                        rinv = sbt(ph, [128, N], F32); otmp = sbt(ph, [128, N], F32)
                        for h in range(CR_H):
                            for mbk in range(2):
                                pt, pk = nextps()
                                S.op('pe', lambda e: e.matmul(pt[:, 0:N], lhsT=mkTx[:, h, mbk * 128:(mbk + 1) * 128], rhs=fmt[:, h, 0:N], start=True, stop=True),
                                     reads=[mkk, 'fmt'], writes=[pk])
                                Pm = Pc[mbk]; Pk = f'Pc{mbk}'
                                S.op('act', lambda e: e.activation(out=Pm[:, :], in_=pt[:, 0:N], func=AF.Exp), reads=[pk], writes=[Pk])
                                S.op('pe', lambda e: e.matmul(psa[0][:, 0:N], lhsT=mvx[:, mbk, h * 128:(h + 1) * 128], rhs=Pm[:, :], start=(mbk == 0), stop=(mbk == 1)),
                                     reads=[mvk, Pk], writes=['psa0'])
                                S.op('pe', lambda e: e.matmul(psa[1][:, 0:N], lhsT=ones_b[:], rhs=Pm[:, :], start=(mbk == 0), stop=(mbk == 1)),
                                     reads=['ones_b', Pk], writes=['psa1'])
                            S.op('dve', lambda e: e.reciprocal(out=rinv[:], in_=psa[1][:, 0:N]), reads=['psa1'], writes=['rinv'])
                            S.op('dve', lambda e: e.tensor_tensor(out=otmp[:], in0=psa[0][:, 0:N], in1=rinv[:], op=ALU.mult), reads=['psa0', 'rinv'], writes=['otmp'])
                            S.op('dve', lambda e: e.tensor_tensor(out=xg[:, 12 + h, 0:N], in0=otmp[:], in1=fmt[:, 4 + h, 0:N], op=ALU.mult), reads=['otmp', 'fmt'], writes=['xg'])
                        S.barrier()

                    if STOPAT < 5:
                        return
                    with ExitStack() as ph:
                        gsb = sbt(ph, [128, N], F32); ptmp = sbt(ph, [128, N], F32)
                        for br, (wb_, nkb, xs0) in enumerate(((w_ba, 6, 0), (w_bs, 6, 6), (w_bc, 4, 12))):
                            for grp in range(4):
                                wvg, wkg = wload(w_in[l], 16, O_GM + br * D + grp * 512, 512)
                                wvb, wkb = wload(wb_[l], nkb, grp * 512, 512)
                                for cc in range(4):
                                    dm = grp * 4 + cc
                                    pt, pk = nextps()
                                    for k in range(16):
                                        S.op('pe', lambda e: e.matmul(pt[:, 0:N], lhsT=wvg[:, k, cc * 128:(cc + 1) * 128], rhs=hT[:, k, 0:N], start=(k == 0), stop=(k == 15)),
                                             reads=['hT', wkg], writes=[pk])
                                    S.op('act', lambda e: e.activation(out=gsb[:], in_=pt[:, 0:N], func=AF.Sigmoid), reads=[pk], writes=['gsb'])
                                    pt2, pk2 = nextps()
                                    for k in range(nkb):
                                        S.op('pe', lambda e: e.matmul(pt2[:, 0:N], lhsT=wvb[:, k, cc * 128:(cc + 1) * 128], rhs=xg[:, xs0 + k, 0:N], start=(k == 0), stop=(k == nkb - 1)),
                                             reads=['xg', wkb], writes=[pk2])
                                    if br == 0:
                                        S.op('dve', lambda e: e.tensor_tensor(out=fmt[:, dm, 0:N], in0=pt2[:, 0:N], in1=gsb[:], op=ALU.mult), reads=[pk2, 'gsb'], writes=['fmt'])
                                    else:
                                        S.op('dve', lambda e: e.tensor_tensor(out=ptmp[:], in0=pt2[:, 0:N], in1=gsb[:], op=ALU.mult), reads=[pk2, 'gsb'], writes=['ptmp'])
                                        S.op('dve', lambda e: e.tensor_tensor(out=fmt[:, dm, 0:N], in0=fmt[:, dm, 0:N], in1=ptmp[:], op=ALU.add), reads=['fmt', 'ptmp'], writes=['fmt'])
                        S.barrier()

                    if STOPAT < 6:
                        return
                    with ExitStack() as ph:
                        xres = [sbt(ph, [128, 512], F32) for _ in range(2)]
                        xi = 0
                        for cg in range(4):
                            wv, wk = wload(w_out[l], 16, cg * 512, 512)
                            for sb in range(nsb):
                                pt, pk = nextps()
                                for k in range(16):
                                    S.op('pe', lambda e: e.matmul(pt[RS, 0:512], lhsT=fmt[:, k, sb * sbw:(sb + 1) * sbw], rhs=wv[:, k, :], start=(k == 0), stop=(k == 15)),
                                         reads=['fmt', wk], writes=[pk])
                                r0 = tok0 + sb * sbw
                                xr = xres[xi]; xrk = f'xres{xi}'; xi ^= 1
                                S.dma('sp', xr[RS, :], xsrc[r0:r0 + sbw, cg * 512:(cg + 1) * 512], writes=[xrk])
                                st, sk = nextstage()
                                S.op('dve', lambda e: e.tensor_tensor(out=st[RS, :], in0=pt[RS, 0:512], in1=xr[RS, :], op=ALU.add), reads=[pk, xrk], writes=[sk])
                                S.dma('sp', xdst[r0:r0 + sbw, cg * 512:(cg + 1) * 512], st[RS, :], reads=[sk])
                        S.barrier()

                def attn_prompt(t, B):
                    fmt, xg, wiT, KT, Vc, kiT = B.fmt, B.xg, B.wiT, B.KT, B.Vc, B.kiT
                    with ExitStack() as ph:
                        score = sbt(ph, [128, SEQ], F32); maskb = sbt(ph, [128, SEQ], BF16)
                        rl = [sbt(ph, [128, 512], F32) for _ in range(2)]
                        lo = sbt(ph, [128, 1], F32); mid = sbt(ph, [128, 1], F32); cnt = sbt(ph, [128, 1], F32); btmp = sbt(ph, [128, 1], F32)
                        PT = [sbt(ph, [128, 4, 128], BF16) for _ in range(2)]
                        rinv = sbt(ph, [128, 128], F32); otmp = sbt(ph, [128, 128], F32)
                        pti = [0]
                        for qb in range(4):
                            i = 4 * t + qb; L = (i + 1) * 128
                            qs = slice(qb * 128, (qb + 1) * 128)
                            for kg in range((L + 511) // 512):
                                k0 = kg * 512; n = min(512, L - k0)
                                for h in range(IDX_H):
                                    hf = h % 2; c8 = 6 + h // 2
                                    pt, pk = nextps()
                                    S.op('pe', lambda e: e.matmul(pt[:, 0:n], lhsT=fmt[64 * hf:64 * hf + 64, c8, qs], rhs=kiT[64 * hf:64 * hf + 64, k0:k0 + n], start=True, stop=True),
                                         reads=['fmt', 'kiT'], writes=[pk])
                                    r = rl[h % 2]; rk = f'rl{h % 2}'
                                    S.op('act', lambda e: e.activation(out=r[:, 0:n], in_=pt[:, 0:n], func=AF.Relu), reads=[pk], writes=[rk])
                                    if h == 0:
                                        S.op('dve', lambda e: e.tensor_scalar(out=score[:, k0:k0 + n], in0=r[:, 0:n], scalar1=wiT[:, qb, 0:1], scalar2=None, op0=ALU.mult),
                                             reads=[rk, 'wiT'], writes=['score'])
                                    else:
                                        S.op('dve', lambda e: e.scalar_tensor_tensor(out=score[:, k0:k0 + n], in0=r[:, 0:n], scalar=wiT[:, qb, h:h + 1], in1=score[:, k0:k0 + n], op0=ALU.mult, op1=ALU.add),
                                             reads=[rk, 'wiT', 'score'], writes=['score'])
                            S.op('dve', lambda e: e.tensor_tensor(out=score[:, i * 128:L], in0=score[:, i * 128:L], in1=cmask[:], op=ALU.add), reads=['score', 'cmask'], writes=['score'])
                            S.op('dve', lambda e: e.memset(lo[:], -BIS_R), writes=['lo'])
                            for it in range(BIS_IT):
                                c = BIS_R / (2 ** it)
                                S.op('dve', lambda e: e.tensor_scalar(out=mid[:], in0=lo[:], scalar1=c, scalar2=None, op0=ALU.add), reads=['lo'], writes=['mid'])
                                S.op('dve', lambda e: e.tensor_scalar(out=maskb[:, 0:L], in0=score[:, 0:L], scalar1=mid[:, 0:1], scalar2=0.0, op0=ALU.is_ge, op1=ALU.add, accum_out=cnt[:]),
                                     reads=['score', 'mid'], writes=['maskb', 'cnt'])
                                S.op('dve', lambda e: e.tensor_scalar(out=btmp[:], in0=cnt[:], scalar1=255.5, scalar2=c, op0=ALU.is_ge, op1=ALU.mult), reads=['cnt'], writes=['btmp'])
                                S.op('dve', lambda e: e.tensor_tensor(out=lo[:], in0=lo[:], in1=btmp[:], op=ALU.add), reads=['lo', 'btmp'], writes=['lo'])
                            S.op('dve', lambda e: e.tensor_scalar(out=maskb[:, 0:L], in0=score[:, 0:L], scalar1=lo[:, 0:1], scalar2=NEG, op0=ALU.is_lt, op1=ALU.mult),
                                 reads=['score', 'lo'], writes=['maskb'])
                            for h in range(ATT_H):
                                acc = psa[h % 2]; ak = f'psa{h % 2}'
                                first = True
                                for g0 in range(0, i + 1, 4):
                                    nb = min(4, i + 1 - g0)
                                    pt, pk = nextps()
                                    for jj in range(nb):
                                        j = g0 + jj
                                        S.op('pe', lambda e: e.matmul(pt[:, jj * 128:(jj + 1) * 128], lhsT=KT[:, h, j * 128:(j + 1) * 128], rhs=fmt[:, h, qs], start=True, stop=False),
                                             reads=['KT', 'fmt'], writes=[pk])
                                        S.op('pe', lambda e: e.matmul(pt[:, jj * 128:(jj + 1) * 128], lhsT=maskb[:, j * 128:(j + 1) * 128], rhs=ident_b[:], start=False, stop=False),
                                             reads=['maskb', 'ident_b'], writes=[pk])
                                        S.op('pe', lambda e: e.matmul(pt[:, jj * 128:(jj + 1) * 128], lhsT=biasT[:, h, min(i - j, 2), :], rhs=ident_b[:], start=False, stop=True),
                                             reads=['biasT', 'ident_b'], writes=[pk])
                                    P = PT[pti[0]]; Pk = f'PT{pti[0]}'; pti[0] ^= 1
                                    S.op('act', lambda e: e.activation(out=P[:, 0:nb, :], in_=pt[:, 0:nb * 128].rearrange("p (j q) -> p j q", q=128), func=AF.Exp), reads=[pk], writes=[Pk])
                                    for jj in range(nb):
                                        j = g0 + jj
                                        last = (j == i)
                                        S.op('pe', lambda e: e.matmul(acc[:, 0:128], lhsT=Vc[:, j, h * 128:(h + 1) * 128], rhs=P[:, jj, :], start=first, stop=last), reads=['Vc', Pk], writes=[ak])
                                        first = False
                                        S.op('pe', lambda e: e.matmul(acc[:, 128:256], lhsT=ones_b[:], rhs=P[:, jj, :], start=False, stop=last), reads=['ones_b', Pk], writes=[ak])
                                S.op('dve', lambda e: e.reciprocal(out=rinv[:], in_=acc[:, 128:256]), reads=[ak], writes=['rinv'])
                                S.op('dve', lambda e: e.tensor_tensor(out=otmp[:], in0=acc[:, 0:128], in1=rinv[:], op=ALU.mult), reads=[ak, 'rinv'], writes=['otmp'])
                                S.op('dve', lambda e: e.tensor_tensor(out=xg[:, h, qs], in0=otmp[:], in1=fmt[:, 14 + h, qs], op=ALU.mult), reads=['otmp', 'fmt'], writes=['xg'])
                        S.barrier()

                def attn_sample(B):
                    fmt, xg, wiT, KTs, Vs, kiTn = B.fmt, B.xg, B.wiT, B.KTs, B.Vs, B.kiTn
                    L = PAST + NS
                    R8 = slice(0, NS)
                    with ExitStack() as ph:
                        score = sbt(ph, [128, L], F32); maskb = sbt(ph, [128, L], BF16)
                        rl = [sbt(ph, [128, 512], F32) for _ in range(2)]
                        lo = sbt(ph, [128, 1], F32); mid = sbt(ph, [128, 1], F32); cnt = sbt(ph, [128, 1], F32); btmp = sbt(ph, [128, 1], F32)
                        kdx = [sbt(ph, [128, 2, 64], F32) for _ in range(2)]
                        kiTc = [sbt(ph, [128, 512], BF16) for _ in range(2)]
                        kpg = [sbt(ph, [128, 768], F32) for _ in range(2)]
                        vpg = [sbt(ph, [128, 768], F32) for _ in range(2)]
                        KTp = [sbt(ph, [128, ATT_H, 128], BF16) for _ in range(2)]
                        Vb = [sbt(ph, [128, 768], BF16) for _ in range(2)]
                        Pp = [sbt(ph, [128, 48], BF16) for _ in range(2)]
                        accS = sbt(ph, [128, ATT_H, 16], F32)
                        rinv = sbt(ph, [128, ATT_H, NS], F32); otmp = sbt(ph, [128, ATT_H, NS], F32)

                        def score_chunk(kview, kkey, k0, n):
                            for h in range(IDX_H):
                                hf = h % 2; c8 = 6 + h // 2
                                pt, pk = nextps()
                                S.op('pe', lambda e: e.matmul(pt[R8, 0:n], lhsT=fmt[64 * hf:64 * hf + 64, c8, 0:NS], rhs=kview[64 * hf:64 * hf + 64, 0:n], start=True, stop=True),
                                     reads=['fmt', kkey], writes=[pk])
                                r = rl[h % 2]; rk = f'rl{h % 2}'
                                S.op('act', lambda e: e.activation(out=r[R8, 0:n], in_=pt[R8, 0:n], func=AF.Relu), reads=[pk], writes=[rk])
                                if h == 0:
                                    S.op('dve', lambda e: e.tensor_scalar(out=score[R8, k0:k0 + n], in0=r[R8, 0:n], scalar1=wiT[R8, 0, 0:1], scalar2=None, op0=ALU.mult),
                                         reads=[rk, 'wiT'], writes=['score'])
                                else:
                                    S.op('dve', lambda e: e.scalar_tensor_tensor(out=score[R8, k0:k0 + n], in0=r[R8, 0:n], scalar=wiT[R8, 0, h:h + 1], in1=score[R8, k0:k0 + n], op0=ALU.mult, op1=ALU.add),
                                         reads=[rk, 'wiT', 'score'], writes=['score'])

                        for kg in range(NPAGE // 4):
                            kc = kiTc[kg % 2]; kck = f'kiTc{kg % 2}'
                            pt, pk = nextps()
                            for pj in range(4):
                                j = kg * 4 + pj
                                kd = kdx[j % 2]; kdk = f'kdx{j % 2}'
                                S.idma(kd[:, 0, :], cki[l][:, :], idx_i[:, j:j + 1], reads=['idx_i'], writes=[kdk])
                                copy('dve', kd[:, 1, :], kd[:, 0, :], [kdk], [kdk])
                                S.op('pe', lambda e: e.transpose(pt[:, pj * 128:(pj + 1) * 128], kd[:].rearrange("p a d -> p (a d)"), ident_f[:]),
                                     reads=[kdk, 'ident_f'], writes=[pk])
                            copy('act', kc[:, :], pt[:, 0:512], [pk], [kck])
                            score_chunk(kc, kck, kg * 512, 512)
                        score_chunk(kiTn, 'kiT', PAST, NS)
                        S.op('dve', lambda e: e.tensor_tensor(out=score[R8, PAST:L], in0=score[R8, PAST:L], in1=cmask[R8, 0:NS], op=ALU.add), reads=['score', 'cmask'], writes=['score'])
                        S.op('dve', lambda e: e.memset(lo[:], -BIS_R), writes=['lo'])
                        for it in range(BIS_IT):
                            c = BIS_R / (2 ** it)
                            S.op('dve', lambda e: e.tensor_scalar(out=mid[R8, :], in0=lo[R8, :], scalar1=c, scalar2=None, op0=ALU.add), reads=['lo'], writes=['mid'])
                            S.op('dve', lambda e: e.tensor_scalar(out=maskb[R8, 0:L], in0=score[R8, 0:L], scalar1=mid[R8, 0:1], scalar2=0.0, op0=ALU.is_ge, op1=ALU.add, accum_out=cnt[R8, :]),
                                 reads=['score', 'mid'], writes=['maskb', 'cnt'])
                            S.op('dve', lambda e: e.tensor_scalar(out=btmp[R8, :], in0=cnt[R8, :], scalar1=255.5, scalar2=c, op0=ALU.is_ge, op1=ALU.mult), reads=['cnt'], writes=['btmp'])
                            S.op('dve', lambda e: e.tensor_tensor(out=lo[R8, :], in0=lo[R8, :], in1=btmp[R8, :], op=ALU.add), reads=['lo', 'btmp'], writes=['lo'])
                        S.op('dve', lambda e: e.tensor_scalar(out=maskb[R8, 0:L], in0=score[R8, 0:L], scalar1=lo[R8, 0:1], scalar2=NEG, op0=ALU.is_lt, op1=ALU.mult),
                             reads=['score', 'lo'], writes=['maskb'])

                        S.op('dve', lambda e: e.memset(accS[:], 0.0), writes=['accS'])

                        def accumulate(ptl, pkl, rows, vview, vkey, kT_of, kkey, mcol0, bias_of, Pt, Ptk):
                            nr = rows.stop
                            for h in range(ATT_H):
                                hs = slice(h * NS, (h + 1) * NS)
                                b_ = bias_of(h)
                                S.op('pe', lambda e: e.matmul(ptl[rows, hs], lhsT=kT_of(h), rhs=fmt[:, h, 0:NS], start=True, stop=False), reads=[kkey, 'fmt'], writes=[pkl])
                                S.op('pe', lambda e: e.matmul(ptl[rows, hs], lhsT=maskb[R8, mcol0:mcol0 + nr], rhs=ident_b[R8, R8], start=False, stop=(b_ is None)),
                                     reads=['maskb', 'ident_b'], writes=[pkl])
                                if b_ is not None:
                                    S.op('pe', lambda e: e.matmul(ptl[rows, hs], lhsT=b_, rhs=ident_b[R8, R8], start=False, stop=True), reads=['biasS', 'ident_b'], writes=[pkl])
                            S.op('act', lambda e: e.activation(out=Pt[rows, :], in_=ptl[rows, 0:ATT_H * NS], func=AF.Exp), reads=[pkl], writes=[Ptk])
                            pto, pko = nextps()
                            for h in range(ATT_H):
                                hs = slice(h * NS, (h + 1) * NS)
                                S.op('pe', lambda e: e.matmul(pto[:, h * 16:h * 16 + NS], lhsT=vview[rows, h * 128:(h + 1) * 128], rhs=Pt[rows, hs], start=True, stop=True), reads=[vkey, Ptk], writes=[pko])
                                S.op('pe', lambda e: e.matmul(pto[:, h * 16 + NS:h * 16 + 16], lhsT=ones_b[rows, :], rhs=Pt[rows, hs], start=True, stop=True), reads=['ones_b', Ptk], writes=[pko])
                            S.op('dve', lambda e: e.tensor_tensor(out=accS[:].rearrange("p h c -> p (h c)"), in0=accS[:].rearrange("p h c -> p (h c)"), in1=pto[:, 0:ATT_H * 16], op=ALU.add),
                                 reads=['accS', pko], writes=['accS'])

                        for j in range(NPAGE):
                            b2 = j % 2
                            S.idma(kpg[b2][:, :], ck[l][:, :], idx_i[:, j:j + 1], reads=['idx_i'], writes=[f'kpg{b2}'])
                            S.idma(vpg[b2][:, :], cv[l][:, :], idx_i[:, j:j + 1], reads=['idx_i'], writes=[f'vpg{b2}'])
                            ptA, pkA = nextps()
                            for hh in range(4):
                                S.op('pe', lambda e: e.transpose(ptA[:, hh * 128:(hh + 1) * 128], kpg[b2][:, hh * 128:(hh + 1) * 128], ident_f[:]), reads=[f'kpg{b2}', 'ident_f'], writes=[pkA])
                            copy('act', KTp[b2][:, 0:4, :], ptA[:, 0:512].rearrange("p (h t) -> p h t", t=128), [pkA], [f'KTp{b2}'])
                            ptB, pkB = nextps()
                            for hh in range(2):
                                S.op('pe', lambda e: e.transpose(ptB[:, hh * 128:(hh + 1) * 128], kpg[b2][:, (4 + hh) * 128:(5 + hh) * 128], ident_f[:]), reads=[f'kpg{b2}', 'ident_f'], writes=[pkB])
                            copy('dve', KTp[b2][:, 4:6, :], ptB[:, 0:256].rearrange("p (h t) -> p h t", t=128), [pkB], [f'KTp{b2}'])
                            copy('pool', Vb[b2][:, :], vpg[b2][:, :], [f'vpg{b2}'], [f'Vb{b2}'])
                            ptl, pkl = nextps()
                            accumulate(ptl, pkl, slice(0, 128), Vb[b2], f'Vb{b2}', lambda h: KTp[b2][:, h, :], f'KTp{b2}', j * 128,
                                       (lambda h: biasS[R8, h, 0:128]) if j == NPAGE - 1 else (lambda h: None), Pp[b2], f'Pp{b2}')
                        ptl, pkl = nextps()
                        accumulate(ptl, pkl, R8, Vs, 'Vc', lambda h: KTs[:, h, 0:NS], 'KT', PAST, lambda h: biasS[R8, h, 128:136], Pp[0], 'Pp0')
                        S.op('dve', lambda e: e.reciprocal(out=rinv[:], in_=accS[:, :, NS:16]), reads=['accS'], writes=['rinv'])
                        S.op('dve', lambda e: e.tensor_tensor(out=otmp[:], in0=accS[:, :, 0:NS], in1=rinv[:], op=ALU.mult), reads=['accS', 'rinv'], writes=['otmp'])
                        S.op('dve', lambda e: e.tensor_tensor(out=xg[:, 0:ATT_H, 0:NS], in0=otmp[:], in1=fmt[:, 14:14 + ATT_H, 0:NS], op=ALU.mult), reads=['otmp', 'fmt'], writes=['xg'])
                        S.barrier()

                if SAMPLE:
                    with ExitStack() as sc:
                        B = NS_()
                        B.hT = sbt(sc, [128, 16, NS], BF16); B.fmt = sbt(sc, [128, 20, NS], BF16); B.xg = sbt(sc, [128, 16, NS], BF16)
                        B.wiT = sbt(sc, [128, 1, 16], F32)
                        B.KTs = sbt(sc, [128, ATT_H, NS], BF16); B.Vs = sbt(sc, [128, 768], BF16); B.kiTn = sbt(sc, [128, NS], BF16)
                        mkT_s = sbt(sc, [128, CR_H, MEM], BF16); mv_s = sbt(sc, [128, 2, 512], BF16)
                        with ExitStack() as ph:
                            mt = sbt(ph, [128, 1024], F32)
                            for mbk in range(2):
                                S.dma('sp', mt[:, 0:512], cmk[l, mbk * 128:(mbk + 1) * 128, :], writes=['mt'])
                                S.dma('sp', mt[:, 512:1024], cmv[l, mbk * 128:(mbk + 1) * 128, :], writes=['mt'])
                                copy('act', mv_s[:, mbk, :], mt[:, 512:1024], ['mt'], ['mv_s'])
                                pt, pk = nextps()
                                for h in range(CR_H):
                                    S.op('pe', lambda e: e.transpose(pt[:, h * 128:(h + 1) * 128], mt[:, h * 128:(h + 1) * 128], ident_f[:]),
                                         reads=['mt', 'ident_f'], writes=[pk])
                                copy('dve', mkT_s[:, :, mbk * 128:(mbk + 1) * 128], pt[:, 0:512].rearrange("p (h t) -> p h t", t=128), [pk], ['mkT_s'])
                            S.barrier()
                        B.mkT_s = mkT_s; B.mv_s = mv_s
                        run_tile(True, 0, B)
                        S.barrier()
                if PROMPT:
                    with ExitStack() as sc:
                        B = NS_()
                        B.hT = sbt(sc, [128, 16, T], BF16); B.fmt = sbt(sc, [128, 20, T], BF16); B.xg = sbt(sc, [128, 16, T], BF16)
                        B.wiT = sbt(sc, [128, 4, 16], F32)
                        B.KT = sbt(sc, [128, ATT_H, SEQ], BF16); B.Vc = sbt(sc, [128, 16, 768], BF16); B.kiT = sbt(sc, [128, SEQ], BF16)
                        for t in range(NT):
                            run_tile(False, t, B)
                        S.barrier()
            S.barrier()

        with ExitStack() as ph:
            fngB = sbt(ph, [128, D], F32); xt = sbt(ph, [128, D], F32); junk = sbt(ph, [128, D], BF16); ssq = sbt(ph, [128, 1], F32)
            yo = [sbt(ph, [128, D], F32) for _ in range(2)]
            S.dma('sp', fngB[:], fng.unsqueeze(0).broadcast_to([128, D]), writes=['fngB'])
            jobs = []
            if PROMPT and STOPAT >= 7:
                jobs += [(False, rb) for rb in range(SEQ // 128)]
            if SAMPLE and STOPAT >= 7:
                jobs += [(True, 0)]
            for ji, (smp, rb) in enumerate(jobs):
                RS = slice(0, NS) if smp else slice(0, 128)
                if smp:
                    S.dma('sp', xt[RS, :], xsscr[DEPTH - 1][:, :], writes=['xt'])
                else:
                    S.dma('sp', xt[:], xscr[DEPTH - 1][rb * 128:(rb + 1) * 128, :], writes=['xt'])
                srct = xt; rk = 'xt'
                rms_rstd(srct[RS, :], junk[RS, :], ssq[RS, :], rk)
                y = yo[ji % 2]; yk = f'yo{ji % 2}'
                S.op('dve', lambda e: e.scalar_tensor_tensor(out=y[RS, :], in0=srct[RS, :], scalar=ssq[RS, 0:1], in1=fngB[RS, :], op0=ALU.mult, op1=ALU.mult),
                     reads=[rk, 'ssq', 'fngB'], writes=[yk])
                S.dma('sp', (ys[:, :] if smp else yp[rb * 128:(rb + 1) * 128, :]), y[RS, :], reads=[yk])
            S.barrier()
        S.barrier()
    return nc


_CACHE = {}


def kernel(**inp):
    f32 = lambda a: np.ascontiguousarray(np.asarray(a), dtype=np.float32)
    if 'nc' not in _CACHE:
        _CACHE['nc'] = build_program()
    nc = _CACHE['nc']
    consts = make_consts()
    shared = {
        "norm_g": f32(inp['norm_g']), "w_in": f32(inp['w_in']), "w_ba": f32(inp['w_branch_attn']),
        "w_bs": f32(inp['w_branch_ssm']), "w_bc": f32(inp['w_branch_cross']), "w_out": f32(inp['w_out']),
        "w_mem": f32(inp['w_mem_kv']), "a_re": f32(inp['ssm_a_re']), "a_im": f32(inp['ssm_a_im']),
        "b_re": f32(inp['ssm_b_re']), "b_im": f32(inp['ssm_b_im']), "c_re": f32(inp['ssm_c_re']), "c_im": f32(inp['ssm_c_im']),
        "ssm_d": f32(inp['ssm_d']), "log_dt": f32(inp['ssm_log_dt']), "w_glu": f32(inp['w_glu']), "b_glu": f32(inp['b_glu']),
        "rel_bias": f32(inp['rel_bias']), "fng": f32(inp['final_norm_g']),
    }
    shared.update(consts)
    cache_k = f32(inp['cache_k']); cache_v = f32(inp['cache_v']); cache_kidx = f32(inp['cache_kidx'])
    for l in range(DEPTH if SAMPLE else 0):
        shared[f"ck{l}"] = cache_k[l].reshape(NPOOL * 128, 768)
        shared[f"cv{l}"] = cache_v[l].reshape(NPOOL * 128, 768)
        shared[f"cki{l}"] = cache_kidx[l].reshape(NPOOL * 128, 64)
    xp_all = f32(inp['x_prompt']); xs_all = f32(inp['x_sample']); mem_all = f32(inp['mem_prompt'])
    cmk_all = f32(inp['cache_mem_k']); cmv_all = f32(inp['cache_mem_v'])
    sre_all = f32(inp['state_ssm_re']); sim_all = f32(inp['state_ssm_im'])
    pgt_all = np.ascontiguousarray(np.asarray(inp['page_table']), dtype=np.int32)
    in_maps = []
    for c in range(8):
        m = dict(shared)
        m["xp"] = xp_all[c % 4]; m["xs"] = xs_all[c]; m["memp"] = mem_all[c % 4]
        m["cmk"] = np.ascontiguousarray(cmk_all[:, c]).reshape(DEPTH, MEM, 512)
        m["cmv"] = np.ascontiguousarray(cmv_all[:, c]).reshape(DEPTH, MEM, 512)
        m["sre"] = np.ascontiguousarray(sre_all[:, c]); m["sim"] = np.ascontiguousarray(sim_all[:, c])
        m["pgt"] = pgt_all[c:c + 1]
        in_maps.append(m)
    res = run_bass_kernel_spmd(nc, in_maps, core_ids=list(range(8)))
    R = res.results
    B = 4
    y_prompt = np.stack([R[b]["yp"] for b in range(B)])
    y_sample = np.stack([R[c]["ys"] for c in range(8)])
    st = lambda name, shp: np.stack([R[b][name] for b in range(B)], axis=1).reshape(shp)
    k_prompt = st("kp", (DEPTH, B, SEQ, ATT_H, 128)); v_prompt = st("vp", (DEPTH, B, SEQ, ATT_H, 128))
    kidx_prompt = st("kip", (DEPTH, B, SEQ, 64))
    mk = st("mkp", (DEPTH, B, MEM, CR_H, 128)); mvv = st("mvp", (DEPTH, B, MEM, CR_H, 128))
    sr = st("srp", (DEPTH, B, 48, 64)); si = st("sip", (DEPTH, B, 48, 64))
    ss = lambda name, shp: np.stack([R[c][name] for c in range(8)], axis=1).reshape(shp)
    k_s = ss("ks", (DEPTH, 8, NS, ATT_H, 128)); v_s = ss("vs", (DEPTH, 8, NS, ATT_H, 128)); ki_s = ss("kis", (DEPTH, 8, NS, 64))
    sr_s = ss("srs", (DEPTH, 8, 48, 64)); si_s = ss("sis", (DEPTH, 8, 48, 64))
    return (y_prompt, y_sample, k_prompt, v_prompt, kidx_prompt, mk, mvv, sr, si, k_s, v_s, ki_s, sr_s, si_s)
```
